# Optimizing a Trainium2 kernel written in Bass

```python
import math
import jax
import jax.numpy as jnp
from jax import lax
import numpy as np

D_MODEL = 2048
BATCH = 1
SEQ = 8192
DEPTH = 4
DEC_BATCH = 1
DEC_SEQ = 16384
PAST_LEN = 128

N_EVEN = (DEPTH + 1) // 2
N_ODD = DEPTH // 2
EPS = 1e-6
Q_BLOCK = 128
GRID_W = 64
N_MEM = 256
D_FF = 5632

S5_WIDTH = D_MODEL // 2
S5_GROUP = 16
S5_GROUPS = S5_WIDTH // S5_GROUP
S5_STATE = 64
S5_DT_MIN = 1e-3
S5_DT_MAX = 1e-1
S5_LAMBDA_RE_MAX = -1e-4

DIFF_WIDTH = D_MODEL - S5_WIDTH
DIFF_HEAD = 64
DIFF_HEADS = DIFF_WIDTH // (2 * DIFF_HEAD)
DIFF_VHEAD = 2 * DIFF_HEAD
DIFF_SUBLN_EPS = 1e-5
EVEN_IN = S5_WIDTH + 3 * DIFF_WIDTH

GQA_HEAD = 128
GQA_HEADS = D_MODEL // GQA_HEAD
GQA_KV_HEADS = 4
GQA_GROUP = GQA_HEADS // GQA_KV_HEADS
ODD_IN = (GQA_HEADS + 2 * GQA_KV_HEADS) * GQA_HEAD
ROPE_AXIS = GQA_HEAD // 2
ROPE_THETA = 10000.0

X_HEADS = 4
X_HEAD = D_MODEL // X_HEADS

kernel_name = "hybrid_s5_diffattn_axial_gqa_encoder"


def rmsnorm(x, g, eps=EPS):
    xf = x.astype(jnp.float32)
    y = xf * lax.rsqrt(jnp.mean(xf * xf, axis=-1, keepdims=True) + eps)
    return (y * g.astype(jnp.float32)).astype(x.dtype)


def swiglu(h, w_gu, w_down):
    g, u = jnp.split(h @ w_gu, 2, axis=-1)
    return (jax.nn.silu(g) * u) @ w_down


def alibi_slopes(n):
    return jnp.asarray([2.0 ** (-8.0 * (i + 1) / n) for i in range(n)], dtype=jnp.float32)


def s5_combine(e1, e2):
    a1r, a1i, b1r, b1i = e1
    a2r, a2i, b2r, b2i = e2
    ar = a2r * a1r - a2i * a1i
    ai = a2r * a1i + a2i * a1r
    br = a2r * b1r - a2i * b1i + b2r
    bi = a2r * b1i + a2i * b1r + b2i
    return (ar, ai, br, bi)


def s5_mixer(u, lam_re, lam_im, log_dt, b_re, b_im, c_re, c_im, d_skip, glu_w, glu_b):
    bsz, L, _ = u.shape
    uf = u.astype(jnp.float32).reshape(bsz, L, S5_GROUPS, S5_GROUP)
    y = d_skip.astype(jnp.float32).reshape(S5_GROUPS, S5_GROUP) * uf
    for direction in range(2):
        lr = jnp.minimum(lam_re[direction].astype(jnp.float32), S5_LAMBDA_RE_MAX)
        li = lam_im[direction].astype(jnp.float32)
        dt = jnp.exp(log_dt[direction].astype(jnp.float32))[:, None]
        mag = jnp.exp(lr * dt)
        ab_re = mag * jnp.cos(li * dt)
        ab_im = mag * jnp.sin(li * dt)
        nr = ab_re - 1.0
        den = lr * lr + li * li
        f_re = ((nr * lr + ab_im * li) / den)[..., None]
        f_im = ((ab_im * lr - nr * li) / den)[..., None]
        br = b_re[direction].astype(jnp.float32)
        bi = b_im[direction].astype(jnp.float32)
        bb_re = f_re * br - f_im * bi
        bb_im = f_re * bi + f_im * br
        bu_re = jnp.einsum('blgh,gph->blgp', uf, bb_re)
        bu_im = jnp.einsum('blgh,gph->blgp', uf, bb_im)
        a_re = jnp.broadcast_to(ab_re, bu_re.shape)
        a_im = jnp.broadcast_to(ab_im, bu_im.shape)
        _, _, x_re, x_im = lax.associative_scan(
            s5_combine, (a_re, a_im, bu_re, bu_im), reverse=(direction == 1), axis=1)
        cr = c_re[direction].astype(jnp.float32)
        ci = c_im[direction].astype(jnp.float32)
        y = y + jnp.einsum('blgp,ghp->blgh', x_re, cr) - jnp.einsum('blgp,ghp->blgh', x_im, ci)
    y = y.reshape(bsz, L, S5_WIDTH).astype(u.dtype)
    g = jax.nn.gelu(y)
    return g * jax.nn.sigmoid(g @ glu_w + glu_b)


def diff_attention(q, k, v, lam, lambda_init, subln_g):
    bsz, H, _, L, d = q.shape
    nb = L // Q_BLOCK
    slopes = alibi_slopes(H)
    kpos = jnp.arange(L, dtype=jnp.int32)
    scale = d ** -0.5
    qb = q.reshape(bsz, H, 2, nb, Q_BLOCK, d).transpose(3, 0, 1, 2, 4, 5)
    starts = jnp.arange(nb, dtype=jnp.int32) * Q_BLOCK

    def block(args):
        qi, s0 = args
        qpos = s0 + jnp.arange(Q_BLOCK, dtype=jnp.int32)
        dist = jnp.abs(qpos[:, None] - kpos[None, :]).astype(jnp.float32)
        bias = -slopes[:, None, None] * dist
        s = jnp.einsum('bhmqd,bhmkd->bhmqk', qi, k,
                       preferred_element_type=jnp.float32) * scale + bias[None, :, None]
        p = jax.nn.softmax(s, axis=-1)
        a = p[:, :, 0] - lam * p[:, :, 1]
        return jnp.einsum('bhqk,bhke->bhqe', a.astype(v.dtype), v)

    o = lax.map(block, (qb, starts))
    o = o.transpose(1, 2, 0, 3, 4).reshape(bsz, H, L, 2 * d)
    return rmsnorm(o, subln_g, eps=DIFF_SUBLN_EPS) * (1.0 - lambda_init)


def even_mixer(h, w_in, w_out, lam_re, lam_im, log_dt, b_re, b_im, c_re, c_im, d_skip,
               glu_w, glu_b, lq1, lk1, lq2, lk2, subln_g, layer_idx):
    bsz, L, _ = h.shape
    z = h @ w_in
    u, q, k, v = jnp.split(z, [S5_WIDTH, S5_WIDTH + DIFF_WIDTH, S5_WIDTH + 2 * DIFF_WIDTH], axis=-1)
    ya = s5_mixer(u, lam_re, lam_im, log_dt, b_re, b_im, c_re, c_im, d_skip, glu_w, glu_b)
    q = q.reshape(bsz, L, DIFF_HEADS, 2, DIFF_HEAD).transpose(0, 2, 3, 1, 4)
    k = k.reshape(bsz, L, DIFF_HEADS, 2, DIFF_HEAD).transpose(0, 2, 3, 1, 4)
    v = v.reshape(bsz, L, DIFF_HEADS, DIFF_VHEAD).transpose(0, 2, 1, 3)
    lambda_init = 0.8 - 0.6 * math.exp(-0.3 * layer_idx)
    lam = (jnp.exp(jnp.sum(lq1.astype(jnp.float32) * lk1.astype(jnp.float32)))
           - jnp.exp(jnp.sum(lq2.astype(jnp.float32) * lk2.astype(jnp.float32))) + lambda_init)
    yb = diff_attention(q, k, v, lam, lambda_init, subln_g)
    yb = yb.transpose(0, 2, 1, 3).reshape(bsz, L, DIFF_WIDTH)
    return jnp.concatenate([ya, yb.astype(ya.dtype)], axis=-1) @ w_out


def axial_rope_tables(L):
    rows = L // GRID_W
    r = jnp.repeat(jnp.arange(rows, dtype=jnp.float32), GRID_W)
    c = jnp.tile(jnp.arange(GRID_W, dtype=jnp.float32), rows)
    inv = ROPE_THETA ** (-jnp.arange(0, ROPE_AXIS, 2, dtype=jnp.float32) / ROPE_AXIS)
    ang = jnp.stack([r[:, None] * inv, c[:, None] * inv], axis=1)
    return jnp.cos(ang), jnp.sin(ang)


def apply_axial_rope(x, cos, sin):
    xs = x.astype(jnp.float32).reshape(x.shape[:-1] + (2, 2, ROPE_AXIS // 2))
    x1 = xs[..., 0, :]
    x2 = xs[..., 1, :]
    o1 = x1 * cos - x2 * sin
    o2 = x2 * cos + x1 * sin
    return jnp.stack([o1, o2], axis=-2).reshape(x.shape).astype(x.dtype)


def gqa_attention(q, k, v):
    bsz, hk, g, L, d = q.shape
    nb = L // Q_BLOCK
    scale = d ** -0.5
    qb = q.reshape(bsz, hk, g, nb, Q_BLOCK, d).transpose(3, 0, 1, 2, 4, 5)

    def block(qi):
        s = jnp.einsum('bhgqd,bhkd->bhgqk', qi, k, preferred_element_type=jnp.float32) * scale
        p = jax.nn.softmax(s, axis=-1)
        return jnp.einsum('bhgqk,bhkd->bhgqd', p.astype(v.dtype), v)

    o = lax.map(block, qb)
    return o.transpose(1, 2, 3, 0, 4, 5).reshape(bsz, hk, g, L, d)


def odd_mixer(h, w_in, w_out, q_g, k_g, cos, sin):
    bsz, L, _ = h.shape
    z = h @ w_in
    q, k, v = jnp.split(z, [GQA_HEADS * GQA_HEAD, (GQA_HEADS + GQA_KV_HEADS) * GQA_HEAD], axis=-1)
    q = rmsnorm(q.reshape(bsz, L, GQA_HEADS, GQA_HEAD), q_g).transpose(0, 2, 1, 3)
    k = rmsnorm(k.reshape(bsz, L, GQA_KV_HEADS, GQA_HEAD), k_g).transpose(0, 2, 1, 3)
    v = v.reshape(bsz, L, GQA_KV_HEADS, GQA_HEAD).transpose(0, 2, 1, 3)
    q = apply_axial_rope(q, cos, sin).reshape(bsz, GQA_KV_HEADS, GQA_GROUP, L, GQA_HEAD)
    k = apply_axial_rope(k, cos, sin)
    o = gqa_attention(q, k, v)
    o = o.transpose(0, 3, 1, 2, 4).reshape(bsz, L, D_MODEL)
    return o @ w_out


def cross_attention(h, m, w_q, w_kv, w_o):
    bsz, L, _ = h.shape
    n = m.shape[1]
    q = (h @ w_q).reshape(bsz, L, X_HEADS, X_HEAD)
    k, v = jnp.split(m @ w_kv, 2, axis=-1)
    k = k.reshape(bsz, n, X_HEADS, X_HEAD)
    v = v.reshape(bsz, n, X_HEADS, X_HEAD)
    s = jnp.einsum('blhd,bnhd->bhln', q, k, preferred_element_type=jnp.float32) * (X_HEAD ** -0.5)
    p = jax.nn.softmax(s, axis=-1)
    o = jnp.einsum('bhln,bnhd->blhd', p.astype(v.dtype), v).reshape(bsz, L, D_MODEL)
    return o @ w_o


def encoder(x, mem, p):
    L = x.shape[1]
    cos, sin = axial_rope_tables(L)
    for l in range(DEPTH):
        x = x + 0.5 * swiglu(rmsnorm(x, p['ffn1_norm'][l]), p['ffn1_w_gu'][l], p['ffn1_w_down'][l])
        h = rmsnorm(x, p['mix_norm'][l])
        if l % 2 == 0:
            e = l // 2
            x = x + even_mixer(h, p['even_w_in'][e], p['even_w_out'][e],
                               p['s5_lambda_re'][e], p['s5_lambda_im'][e], p['s5_log_dt'][e],
                               p['s5_b_re'][e], p['s5_b_im'][e], p['s5_c_re'][e], p['s5_c_im'][e],
                               p['s5_d'][e], p['s5_glu_w'][e], p['s5_glu_b'][e],
                               p['diff_lambda_q1'][e], p['diff_lambda_k1'][e],
                               p['diff_lambda_q2'][e], p['diff_lambda_k2'][e],
                               p['diff_subln'][e], l)
        else:
            o = l // 2
            x = x + odd_mixer(h, p['odd_w_in'][o], p['odd_w_out'][o],
                              p['gqa_q_norm'][o], p['gqa_k_norm'][o], cos, sin)
        x = x + cross_attention(rmsnorm(x, p['cross_norm'][l]), rmsnorm(mem, p['mem_norm'][l]),
                                p['cross_w_q'][l], p['cross_w_kv'][l], p['cross_w_o'][l])
        x = x + 0.5 * swiglu(rmsnorm(x, p['ffn2_norm'][l]), p['ffn2_w_gu'][l], p['ffn2_w_down'][l])
    return rmsnorm(x, p['final_norm'])


def _dense(k, shape, fan_in):
    return jax.random.normal(k, shape, jnp.float32) * (fan_in ** -0.5)


def _gain(k, shape):
    return 1.0 + 0.02 * jax.random.normal(k, shape, jnp.float32)


def setup_inputs(seed: int = 0) -> dict:
    key = jax.random.key(seed)
    ks = list(jax.random.split(key, 48))
    G, P, H = S5_GROUPS, S5_STATE, S5_GROUP
    n_idx = jnp.arange(P, dtype=jnp.float32)
    d = {}
    d['x_prompt'] = jax.random.normal(ks[0], (BATCH, SEQ, D_MODEL), jnp.float32)
    d['x_sample'] = jax.random.normal(ks[1], (DEC_BATCH, DEC_SEQ, D_MODEL), jnp.float32)
    d['mem_prompt'] = jax.random.normal(ks[2], (BATCH, N_MEM, D_MODEL), jnp.float32)
    d['mem_sample'] = jax.random.normal(ks[3], (DEC_BATCH, N_MEM, D_MODEL), jnp.float32)
    d['ffn1_norm'] = _gain(ks[4], (DEPTH, D_MODEL))
    d['ffn1_w_gu'] = _dense(ks[5], (DEPTH, D_MODEL, 2 * D_FF), D_MODEL)
    d['ffn1_w_down'] = _dense(ks[6], (DEPTH, D_FF, D_MODEL), D_FF)
    d['mix_norm'] = _gain(ks[7], (DEPTH, D_MODEL))
    d['even_w_in'] = _dense(ks[8], (N_EVEN, D_MODEL, EVEN_IN), D_MODEL)
    d['even_w_out'] = _dense(ks[9], (N_EVEN, D_MODEL, D_MODEL), D_MODEL)
    d['s5_lambda_re'] = -0.5 + 0.01 * jax.random.normal(ks[10], (N_EVEN, 2, G, P), jnp.float32)
    d['s5_lambda_im'] = math.pi * n_idx + 0.01 * jax.random.normal(ks[11], (N_EVEN, 2, G, P), jnp.float32)
    d['s5_log_dt'] = jax.random.uniform(ks[12], (N_EVEN, 2, G), jnp.float32,
                                        math.log(S5_DT_MIN), math.log(S5_DT_MAX))
    d['s5_b_re'] = _dense(ks[13], (N_EVEN, 2, G, P, H), 2 * H)
    d['s5_b_im'] = _dense(ks[14], (N_EVEN, 2, G, P, H), 2 * H)
    d['s5_c_re'] = _dense(ks[15], (N_EVEN, 2, G, H, P), 2 * P)
    d['s5_c_im'] = _dense(ks[16], (N_EVEN, 2, G, H, P), 2 * P)
    d['s5_d'] = jax.random.normal(ks[17], (N_EVEN, S5_WIDTH), jnp.float32)
    d['s5_glu_w'] = _dense(ks[18], (N_EVEN, S5_WIDTH, S5_WIDTH), S5_WIDTH)
    d['s5_glu_b'] = 0.01 * jax.random.normal(ks[19], (N_EVEN, S5_WIDTH), jnp.float32)
    d['diff_lambda_q1'] = 0.1 * jax.random.normal(ks[20], (N_EVEN, DIFF_HEAD), jnp.float32)
    d['diff_lambda_k1'] = 0.1 * jax.random.normal(ks[21], (N_EVEN, DIFF_HEAD), jnp.float32)
    d['diff_lambda_q2'] = 0.1 * jax.random.normal(ks[22], (N_EVEN, DIFF_HEAD), jnp.float32)
    d['diff_lambda_k2'] = 0.1 * jax.random.normal(ks[23], (N_EVEN, DIFF_HEAD), jnp.float32)
    d['diff_subln'] = _gain(ks[24], (N_EVEN, DIFF_VHEAD))
    d['odd_w_in'] = _dense(ks[25], (N_ODD, D_MODEL, ODD_IN), D_MODEL)
    d['odd_w_out'] = _dense(ks[26], (N_ODD, D_MODEL, D_MODEL), D_MODEL)
    d['gqa_q_norm'] = _gain(ks[27], (N_ODD, GQA_HEAD))
    d['gqa_k_norm'] = _gain(ks[28], (N_ODD, GQA_HEAD))
    d['cross_norm'] = _gain(ks[29], (DEPTH, D_MODEL))
    d['mem_norm'] = _gain(ks[30], (DEPTH, D_MODEL))
    d['cross_w_q'] = _dense(ks[31], (DEPTH, D_MODEL, D_MODEL), D_MODEL)
    d['cross_w_kv'] = _dense(ks[32], (DEPTH, D_MODEL, 2 * D_MODEL), D_MODEL)
    d['cross_w_o'] = _dense(ks[33], (DEPTH, D_MODEL, D_MODEL), D_MODEL)
    d['ffn2_norm'] = _gain(ks[34], (DEPTH, D_MODEL))
    d['ffn2_w_gu'] = _dense(ks[35], (DEPTH, D_MODEL, 2 * D_FF), D_MODEL)
    d['ffn2_w_down'] = _dense(ks[36], (DEPTH, D_FF, D_MODEL), D_FF)
    d['final_norm'] = _gain(ks[37], (D_MODEL,))
    return d


def reference(x_prompt, x_sample, mem_prompt, mem_sample,
              ffn1_norm, ffn1_w_gu, ffn1_w_down, mix_norm,
              even_w_in, even_w_out, s5_lambda_re, s5_lambda_im, s5_log_dt,
              s5_b_re, s5_b_im, s5_c_re, s5_c_im, s5_d, s5_glu_w, s5_glu_b,
              diff_lambda_q1, diff_lambda_k1, diff_lambda_q2, diff_lambda_k2, diff_subln,
              odd_w_in, odd_w_out, gqa_q_norm, gqa_k_norm,
              cross_norm, mem_norm, cross_w_q, cross_w_kv, cross_w_o,
              ffn2_norm, ffn2_w_gu, ffn2_w_down, final_norm):
    p = dict(ffn1_norm=ffn1_norm, ffn1_w_gu=ffn1_w_gu, ffn1_w_down=ffn1_w_down, mix_norm=mix_norm,
             even_w_in=even_w_in, even_w_out=even_w_out, s5_lambda_re=s5_lambda_re,
             s5_lambda_im=s5_lambda_im, s5_log_dt=s5_log_dt, s5_b_re=s5_b_re, s5_b_im=s5_b_im,
             s5_c_re=s5_c_re, s5_c_im=s5_c_im, s5_d=s5_d, s5_glu_w=s5_glu_w, s5_glu_b=s5_glu_b,
             diff_lambda_q1=diff_lambda_q1, diff_lambda_k1=diff_lambda_k1,
             diff_lambda_q2=diff_lambda_q2, diff_lambda_k2=diff_lambda_k2, diff_subln=diff_subln,
             odd_w_in=odd_w_in, odd_w_out=odd_w_out, gqa_q_norm=gqa_q_norm, gqa_k_norm=gqa_k_norm,
             cross_norm=cross_norm, mem_norm=mem_norm, cross_w_q=cross_w_q, cross_w_kv=cross_w_kv,
             cross_w_o=cross_w_o, ffn2_norm=ffn2_norm, ffn2_w_gu=ffn2_w_gu, ffn2_w_down=ffn2_w_down,
             final_norm=final_norm)
    y_prompt = encoder(x_prompt, mem_prompt, p)
    y_sample = encoder(x_sample, mem_sample, p)
    return (y_prompt, y_sample)
```

```python
import numpy as np
import concourse.bass as bass
import concourse.mybir as mybir
from concourse.bass_utils import run_bass_kernel_spmd

F32 = mybir.dt.float32
BF16 = mybir.dt.bfloat16
AF = mybir.ActivationFunctionType
ALU = mybir.AluOpType

D = 2048
DC = 16
DFF = 5632
FC = 44
NMEM = 256
EPS = 1e-6
NCORE = 8
T = 512
SEM_LIM = 60000


class Ev:
    __slots__ = ("sem", "val")

    def __init__(self, sem, val):
        self.sem = sem
        self.val = val


class Buf:
    __slots__ = ("name", "w", "r")

    def __init__(self, name):
        self.name = name
        self.w = None
        self.r = []


class Eng:
    def __init__(self, P, name, h):
        self.P = P
        self.name = name
        self.h = h
        self.seen = {}
        self.sem = None
        self.cnt = 0
        self.nsem = 0
        self.pool = []
        self.pk = 0

    def new_sem(self):
        self.sem = self.P.nc.alloc_semaphore(f"s_{self.name}_{self.nsem}")
        self.nsem += 1
        self.cnt = 0

    def wait(self, ev):
        if ev is None:
            return
        k = ev.sem
        if self.seen.get(k, 0) >= ev.val:
            return
        self.h.wait_ge(ev.sem, ev.val)
        self.seen[k] = ev.val

    def mark(self, ins):
        if self.sem is None or self.cnt >= SEM_LIM:
            self.new_sem()
        self.cnt += 1
        ins.then_inc(self.sem, 1)
        return Ev(self.sem, self.cnt)

    def dma_mark_prepare(self):
        NP = 8
        if len(self.pool) < NP:
            self.pool.append([self.P.nc.alloc_semaphore(f"d_{self.name}_{len(self.pool)}_{self.nsem}"), 0])
            self.nsem += 1
            k = len(self.pool) - 1
        else:
            k = self.pk
            self.pk = (self.pk + 1) % NP
        ent = self.pool[k]
        if ent[1] > 0:
            self.wait(Ev(ent[0], ent[1]))
        if ent[1] + 16 > SEM_LIM:
            ent[0] = self.P.nc.alloc_semaphore(f"d_{self.name}_{k}_{self.nsem}")
            self.nsem += 1
            ent[1] = 0
        return ent

    def dma_mark(self, ent, ins):
        ent[1] += 16
        ins.then_inc(ent[0], 16)
        return Ev(ent[0], ent[1])


class Prog:
    def __init__(self, nc):
        self.nc = nc
        self.pe = Eng(self, "pe", nc.tensor)
        self.act = Eng(self, "act", nc.scalar)
        self.dve = Eng(self, "dve", nc.vector)
        self.sp = Eng(self, "sp", nc.sync)
        self.pool = Eng(self, "pool", nc.gpsimd)
        self.engs = [self.pe, self.act, self.dve, self.sp, self.pool]

    def _deps(self, eng, reads, writes):
        best = {}
        for b in reads:
            if b.w is not None:
                best[b.w.sem] = max(best.get(b.w.sem, 0), b.w.val)
        for b in writes:
            if b.w is not None:
                best[b.w.sem] = max(best.get(b.w.sem, 0), b.w.val)
            for e in b.r:
                best[e.sem] = max(best.get(e.sem, 0), e.val)
        for s, v in best.items():
            if eng.name == "pe" and eng.sem is not None and s == eng.sem:
                continue
            eng.wait(Ev(s, v))

    def _record(self, ev, reads, writes):
        for b in reads:
            nr = [e for e in b.r if e.sem is not ev.sem and e.sem != ev.sem]
            nr.append(ev)
            b.r = nr
        for b in writes:
            b.w = ev
            b.r = []

    def op(self, eng, fn, reads=(), writes=(), mark=True):
        self._deps(eng, reads, writes)
        ins = fn()
        if mark:
            ev = eng.mark(ins)
            self._record(ev, reads, writes)
            return ev
        return None

    def pe_deps(self, reads, writes):
        self._deps(self.pe, reads, writes)

    def pe_done(self, ins, reads, writes):
        ev = self.pe.mark(ins)
        self._record(ev, reads, writes)
        return ev

    def dma(self, eng, out, in_, reads=(), writes=(), slow=False):
        ent = eng.dma_mark_prepare()
        self._deps(eng, reads, writes)
        if slow:
            ins = eng.h.dma_start(out=out, in_=in_, allow_slow_non_contiguous=True)
        else:
            ins = eng.h.dma_start(out=out, in_=in_)
        ev = eng.dma_mark(ent, ins)
        self._record(ev, reads, writes)
        return ev

    def barrier(self):
        evs = []
        for e in self.engs:
            if e.sem is not None and e.cnt > 0:
                evs.append(Ev(e.sem, e.cnt))
            for ent in e.pool:
                if ent[1] > 0:
                    evs.append(Ev(ent[0], ent[1]))
        for e in self.engs:
            for ev in evs:
                e.wait(ev)


class WRing:
    def __init__(self, P, tile, nslots, slot_elems):
        self.P = P
        self.tile = tile
        self.n = nslots
        self.se = slot_elems
        self.bufs = [Buf(f"w{i}") for i in range(nslots)]
        self.queue = []
        self.loaded = []
        self.next_slot = 0
        self.issued = 0
        self.consumed = 0

    def enqueue(self, parts):
        self.queue.append(parts)

    def _issue_one(self):
        parts = self.queue[self.issued]
        s = self.issued % self.n
        b = self.bufs[s]
        off = 0
        views = []
        for (kc, ncols, src) in parts:
            v = self.tile[:, s, off:off + kc * ncols].rearrange("p (k c) -> p k c", c=ncols)
            self.P.dma(self.P.pool, v, src, writes=[b])
            views.append(v)
            off += kc * ncols
        assert off <= self.se, (off, self.se)
        self.loaded.append((s, views))
        self.issued += 1

    def get(self):
        while self.issued < len(self.queue) and self.issued < self.consumed + self.n - 1:
            self._issue_one()
        s, views = self.loaded[self.consumed]
        self.consumed += 1
        return self.bufs[s], views


class PSum:
    def __init__(self, P, tile, banks=None):
        self.P = P
        self.tile = tile
        self.banks = list(range(8)) if banks is None else list(banks)
        self.bufs = {i: Buf(f"ps{i}") for i in range(8)}
        self.k = 0

    def get(self):
        k = self.banks[self.k]
        self.k = (self.k + 1) % len(self.banks)
        return self.bufs[k], self.tile[:, k * 512:(k + 1) * 512]

    def fixed(self, k):
        return self.bufs[k], self.tile[:, k * 512:(k + 1) * 512]


class Kern:
    def __init__(self, LP, LS, depth=4, stop=None):
        self.LP, self.LS, self.depth, self.stop = LP, LS, depth, stop
        self.TP, self.TS = LP // NCORE, LS // NCORE
        self.NTOK = self.TP + self.TS
        self.ntile = self.NTOK // T
        assert self.TP % T == 0 and self.TS % T == 0
        nc = bass.Bass("TRN2", target_bir_lowering=False)
        self.nc = nc
        self.P = Prog(nc)
        self.inp = {}

    def din(self, name, shape, dt=F32):
        t = self.nc.dram_tensor(name, list(shape), dt, kind="ExternalInput").ap()
        self.inp[name] = t
        return t

    def load_small(self, dst, src, buf):
        return self.P.dma(self.P.sp, dst, src, writes=[buf])

    def rmsnorm(self, gcol, gbuf):
        P, nc = self.P, self.nc
        bb, bank = self.ps.get()
        for c in range(DC):
            sqb = self.sqb[c % 2]
            sq = self.sq[:, c % 2, :]
            P.op(P.act, lambda: nc.scalar.activation(sq, self.x[:, c, :], AF.Square),
                 reads=[self.xb[c]], writes=[sqb])
            P.pe_deps([sqb, self.cb], [bb] if c == 0 else [])
            ins = nc.tensor.matmul(bank, self.ones[:], sq, start=(c == 0), stop=(c == DC - 1))
            P.pe_done(ins, [sqb], [bb] if c == DC - 1 else [])
        rs = self.rstd[:, :]
        P.op(P.act, lambda: nc.scalar.activation(rs, bank, AF.Sqrt, bias=self.epsc[:, 0:1], scale=1.0 / D),
             reads=[bb, self.cb], writes=[self.rsb])
        P.op(P.dve, lambda: nc.vector.reciprocal(rs, rs), reads=[self.rsb], writes=[self.rsb])
        for c in range(DC):
            P.op(P.dve, lambda: nc.vector.scalar_tensor_tensor(
                self.xn[:, c, :], self.x[:, c, :], gcol[:, c:c + 1], rs, ALU.mult, ALU.mult),
                reads=[self.xb[c], self.rsb, gbuf], writes=[self.xnb[c]])

    def proj(self, src, srcb, kc_n, w_ap, ncols_total, gran, epilogue, col0=0):
        P, nc = self.P, self.nc
        ng = ncols_total // gran
        for g in range(ng):
            wb, (wv,) = self.wr.get()
            for oi in range(gran // 128):
                bb, bank = self.ps.get()
                P.pe_deps([wb] + [srcb[k] for k in range(kc_n)], [bb])
                for k in range(kc_n):
                    ins = nc.tensor.matmul(bank, wv[:, k, oi * 128:(oi + 1) * 128], src[:, k, :],
                                           start=(k == 0), stop=(k == kc_n - 1))
                P.pe_done(ins, [wb] + [srcb[k] for k in range(kc_n)], [bb])
                epilogue(g * (gran // 128) + oi, bb, bank)

    def enq_proj(self, w_ap, kc_n, col0, ncols_total, gran):
        for g in range(ncols_total // gran):
            src = w_ap[:, col0 + g * gran: col0 + (g + 1) * gran].rearrange("(k p) c -> p k c", p=128)
            self.wr.enqueue([(kc_n, gran, src)])

    def enq_ffn(self, wgu, wdn):
        for j in range(FC // 2):
            self.enq_proj(wgu, DC, j * 256, 256, 256)
            self.enq_proj(wgu, DC, DFF + j * 256, 256, 256)
        self.enq_proj(wdn, FC, 0, D, 128)

    def ffn(self, gcol, gbuf, wgu, wdn):
        P, nc = self.P, self.nc
        self.rmsnorm(gcol, gbuf)
        for j2 in range(FC // 2):
            wgb, (wg,) = self.wr.get()
            wub, (wu,) = self.wr.get()
            for oi in range(2):
                j = j2 * 2 + oi
                gb, gbank = self.ps.get()
                ub, ubank = self.ps.get()
                P.pe_deps([wgb] + self.xnb, [gb])
                for k in range(DC):
                    ins = nc.tensor.matmul(gbank, wg[:, k, oi * 128:(oi + 1) * 128], self.xn[:, k, :],
                                           start=(k == 0), stop=(k == DC - 1))
                P.pe_done(ins, [wgb] + self.xnb, [gb])
                P.pe_deps([wub] + self.xnb, [ub])
                for k in range(DC):
                    ins = nc.tensor.matmul(ubank, wu[:, k, oi * 128:(oi + 1) * 128], self.xn[:, k, :],
                                           start=(k == 0), stop=(k == DC - 1))
                P.pe_done(ins, [wub] + self.xnb, [ub])
                tb = self.tmpb[j % 2]
                tmp = self.tmp[:, j % 2, :]
                P.op(P.act, lambda: nc.scalar.activation(tmp, gbank, AF.Silu), reads=[gb], writes=[tb])
                P.op(P.dve, lambda: nc.vector.tensor_tensor(self.sc[:, j, :], tmp, ubank, ALU.mult),
                     reads=[tb, ub], writes=[self.scb[j]])
        for i in range(DC):
            wb, (wv,) = self.wr.get()
            bb, bank = self.ps.get()
            P.pe_deps([wb] + self.scb[:FC], [bb])
            for j in range(FC):
                ins = nc.tensor.matmul(bank, wv[:, j, :], self.sc[:, j, :], start=(j == 0), stop=(j == FC - 1))
            P.pe_done(ins, [wb] + self.scb[:FC], [bb])
            P.op(P.dve, lambda: nc.vector.scalar_tensor_tensor(
                self.x[:, i, :], bank, 0.5, self.x[:, i, :], ALU.mult, ALU.add),
                reads=[bb, self.xb[i]], writes=[self.xb[i]])

    def load_x_tok(self, ti, src=None):
        P, nc = self.P, self.nc
        for tb in range(T // 128):
            r0 = ti * T + tb * 128
            P.dma(P.sp, self.tok[:, :], (self.x_tok if src is None else src)[r0:r0 + 128, :], writes=[self.tokb])
            for c4 in range(DC // 4):
                bb, bank = self.ps.get()
                P.pe_deps([self.tokb, self.cb], [bb])
                for q in range(4):
                    c = c4 * 4 + q
                    ins = nc.tensor.transpose(bank[:, q * 128:(q + 1) * 128], self.tok[:, c * 128:(c + 1) * 128],
                                              self.ident[:])
                P.pe_done(ins, [self.tokb], [bb])
                P.op(P.dve, lambda: nc.vector.tensor_copy(
                    self.x[:, c4 * 4:(c4 + 1) * 4, tb * 128:(tb + 1) * 128],
                    bank.rearrange("p (q t) -> p q t", q=4)),
                    reads=[bb], writes=[self.xb[c4 * 4 + q] for q in range(4)])

    def store_y_tok(self, ti, gcol, gbuf):
        P, nc = self.P, self.nc
        bb, bank = self.ps.get()
        for c in range(DC):
            sqb = self.sqb[c % 2]
            sq = self.sq[:, c % 2, :]
            P.op(P.act, lambda: nc.scalar.activation(sq, self.x[:, c, :], AF.Square),
                 reads=[self.xb[c]], writes=[sqb])
            P.pe_deps([sqb, self.cb], [bb] if c == 0 else [])
            ins = nc.tensor.matmul(bank, self.ones[:], sq, start=(c == 0), stop=(c == DC - 1))
            P.pe_done(ins, [sqb], [bb] if c == DC - 1 else [])
        rs = self.rstd[:, :]
        P.op(P.act, lambda: nc.scalar.activation(rs, bank, AF.Sqrt, bias=self.epsc[:, 0:1], scale=1.0 / D),
             reads=[bb, self.cb], writes=[self.rsb])
        P.op(P.dve, lambda: nc.vector.reciprocal(rs, rs), reads=[self.rsb], writes=[self.rsb])
        for c in range(DC):
            P.op(P.dve, lambda: nc.vector.scalar_tensor_tensor(
                self.x[:, c, :], self.x[:, c, :], gcol[:, c:c + 1], rs, ALU.mult, ALU.mult),
                reads=[self.xb[c], self.rsb, gbuf], writes=[self.xb[c]])
        for tb in range(T // 128):
            r0 = ti * T + tb * 128
            for c4 in range(DC // 4):
                bb, bank = self.ps.get()
                P.pe_deps([self.xb[c4 * 4 + q] for q in range(4)] + [self.cb], [bb])
                for q in range(4):
                    c = c4 * 4 + q
                    ins = nc.tensor.transpose(bank[:, q * 128:(q + 1) * 128], self.x[:, c, tb * 128:(tb + 1) * 128],
                                              self.ident[:])
                P.pe_done(ins, [self.xb[c4 * 4 + q] for q in range(4)], [bb])
                P.op(P.dve, lambda: nc.vector.tensor_copy(self.tok[:, c4 * 512:(c4 + 1) * 512], bank),
                     reads=[bb], writes=[self.tokb])
            P.dma(P.sp, self.y_tok[r0:r0 + 128, :], self.tok[:, :], reads=[self.tokb])

    def enq_memkv(self, wkv):
        self.enq_proj(wkv, DC, 0, D, 256)
        self.enq_proj(wkv, DC, D, D, 256)

    def memkv(self, gcol, gbuf, wkv):
        P, nc = self.P, self.nc
        self.load_x_tok(0, src=self.memin)
        self.rmsnorm(gcol, gbuf)

        def epi_k(oc, bb, bank):
            P.op(P.act, lambda: nc.scalar.copy(self.memK[:, oc, :], bank), reads=[bb], writes=[self.memKb])
        self.proj(self.xn, self.xnb, DC, None, D, 256, epi_k)
        for g in range(D // 256):
            wb, (wv,) = self.wr.get()
            for nb in range(4):
                bb, bank = self.ps.get()
                P.pe_deps([wb] + self.xnb, [bb])
                for k in range(DC):
                    ins = nc.tensor.matmul(bank[:, 0:256], self.xn[:, k, nb * 128:(nb + 1) * 128], wv[:, k, :],
                                           start=(k == 0), stop=(k == DC - 1))
                P.pe_done(ins, [wb] + self.xnb, [bb])
                P.op(P.act, lambda: nc.scalar.copy(self.memV[:, nb, g * 256:(g + 1) * 256], bank[:, 0:256]),
                     reads=[bb], writes=[self.memVb])

    def enq_cross(self, wq, wo):
        self.enq_proj(wq, DC, 0, D, 256)
        self.enq_proj(wo, DC, 0, D, 256)

    def cross(self, ti, gcol, gbuf):
        P, nc = self.P, self.nc
        m0 = 0 if ti < self.TP // T else 2
        self.rmsnorm(gcol, gbuf)
        qT = self.sc
        OT0 = 16

        def epi_q(oc, bb, bank):
            P.op(P.act, lambda: nc.scalar.copy(qT[:, oc, :], bank), reads=[bb], writes=[self.scb[oc]])
        self.proj(self.xn, self.xnb, DC, None, D, 256, epi_q)
        scale = 512.0 ** -0.5
        for h in range(4):
            eb = [self.scb[32 + (h % 2) * 2 + nb] for nb in range(2)]
            E = [self.sc[:, 32 + (h % 2) * 2 + nb, :] for nb in range(2)]
            for nb in range(2):
                bb, bank = self.ps.get()
                rd = [self.memKb] + [self.scb[h * 4 + dc] for dc in range(4)]
                P.pe_deps(rd, [bb])
                for dc in range(4):
                    ins = nc.tensor.matmul(bank, self.memK[:, h * 4 + dc, (m0 + nb) * 128:(m0 + nb + 1) * 128],
                                           qT[:, h * 4 + dc, :], start=(dc == 0), stop=(dc == 3))
                P.pe_done(ins, rd, [bb])
                P.op(P.act, lambda: nc.scalar.activation(E[nb], bank, AF.Exp, scale=scale),
                     reads=[bb], writes=[eb[nb]])
            zb, zbank = self.ps.get()
            P.pe_deps(eb + [self.cb2], [zb])
            for nb in range(2):
                ins = nc.tensor.matmul(zbank, self.onesb[:], E[nb], start=(nb == 0), stop=(nb == 1))
            P.pe_done(ins, eb, [zb])
            tb = self.tmpb[h % 2]
            rz = self.tmp[:, h % 2, :]
            P.op(P.dve, lambda: nc.vector.reciprocal(rz, zbank), reads=[zb], writes=[tb])
            for dc in range(4):
                bb, bank = self.ps.get()
                P.pe_deps(eb + [self.memVb], [bb])
                for nb in range(2):
                    ins = nc.tensor.matmul(bank, self.memV[:, m0 + nb, h * 512 + dc * 128: h * 512 + (dc + 1) * 128],
                                           E[nb], start=(nb == 0), stop=(nb == 1))
                P.pe_done(ins, eb + [self.memVb], [bb])
                oc = OT0 + h * 4 + dc
                P.op(P.dve, lambda: nc.vector.tensor_tensor(self.sc[:, oc, :], bank, rz, ALU.mult),
                     reads=[bb, tb], writes=[self.scb[oc]])

        def epi_o(oc, bb, bank):
            P.op(P.dve, lambda: nc.vector.tensor_tensor(self.x[:, oc, :], bank, self.x[:, oc, :], ALU.add),
                 reads=[bb, self.xb[oc]], writes=[self.xb[oc]])
        self.proj(self.sc[:, OT0:OT0 + DC, :], self.scb[OT0:OT0 + DC], DC, None, D, 256, epi_o)

    def enq_mixout(self, even):
        if even:
            self.enq_proj(self.W["s5_glu_w"], 8, 0, 1024, 256)
        self.enq_proj(self.W["mix_w_out"], DC, 0, D, 256)

    def mixout(self, ti, even):
        P, nc = self.P, self.nc
        ymix = self.sc
        tsl = slice(ti * T, (ti + 1) * T)
        if even:
            for c in range(8):
                P.dma(P.sp, ymix[:, 8 + c, :], self.yb_in[c, :, tsl], writes=[self.scb[8 + c]])
            G0 = 16
            for c in range(8):
                tb = self.tmpb[c % 2]
                tmp = self.tmp[:, c % 2, :]
                P.dma(P.sp, tmp, self.ys5_in[c, :, tsl], writes=[tb])
                P.op(P.act, lambda: nc.scalar.activation(self.sc[:, G0 + c, :], tmp, AF.Gelu),
                     reads=[tb], writes=[self.scb[G0 + c]])

            def epi_g(oc, bb, bank):
                tb = self.tmpb[oc % 2]
                tmp = self.tmp[:, oc % 2, :]
                P.op(P.act, lambda: nc.scalar.activation(tmp, bank, AF.Sigmoid, bias=self.glub[:, oc:oc + 1]),
                     reads=[bb, self.smallb], writes=[tb])
                P.op(P.dve, lambda: nc.vector.tensor_tensor(ymix[:, oc, :], tmp, self.sc[:, G0 + oc, :], ALU.mult),
                     reads=[tb, self.scb[G0 + oc]], writes=[self.scb[oc]])
            self.proj(self.sc[:, G0:G0 + 8, :], self.scb[G0:G0 + 8], 8, None, 1024, 256, epi_g)
        else:
            for c in range(DC):
                P.dma(P.sp, ymix[:, c, :], self.yb_in[c, :, tsl], writes=[self.scb[c]])

        def epi_o(oc, bb, bank):
            P.op(P.dve, lambda: nc.vector.tensor_tensor(self.x[:, oc, :], bank, self.x[:, oc, :], ALU.add),
                 reads=[bb, self.xb[oc]], writes=[self.xb[oc]])
        self.proj(ymix[:, 0:DC, :], self.scb[0:DC], DC, None, D, 256, epi_o)

    def sincos(self, ang, angb, wk, wkb, sin_out, cos_out, outb):
        P, nc = self.P, self.nc
        MAGIC = 12582912.0
        PI = float(np.pi)
        P.op(P.dve, lambda: nc.vector.tensor_scalar(wk, ang, 1.0 / (2 * PI), MAGIC, ALU.mult, ALU.add),
             reads=[angb], writes=[wkb])
        P.op(P.dve, lambda: nc.vector.tensor_scalar(wk, wk, MAGIC, -2 * PI, ALU.subtract, ALU.mult),
             reads=[wkb], writes=[wkb])
        P.op(P.dve, lambda: nc.vector.tensor_tensor(ang, ang, wk, ALU.add), reads=[angb, wkb], writes=[angb])
        P.op(P.act, lambda: nc.scalar.activation(sin_out, ang, AF.Sin), reads=[angb], writes=[outb])
        P.op(P.act, lambda: nc.scalar.activation(wk, ang, AF.Abs), reads=[angb], writes=[wkb])
        P.op(P.act, lambda: nc.scalar.activation(cos_out, wk, AF.Sin, bias=self.halfpi[:, 0:1], scale=-1.0),
             reads=[wkb, self.cb], writes=[outb])

    def enq_inproj(self, even):
        self.enq_proj(self.W["mix_w_in"], DC, 0, 4096 if even else 3072, 256)

    def out_chunk(self, dst_ap, src_bank, bb, k):
        P, nc = self.P, self.nc
        sb_, st = self.stgb[k % 2], self.stg[:, k % 2, :]
        P.op(P.act, lambda: nc.scalar.copy(st, src_bank), reads=[bb], writes=[sb_])
        P.dma(P.sp, dst_ap, st, reads=[sb_])

    def inproj_even(self, ti):
        P, nc = self.P, self.nc
        tsl = slice(ti * T, (ti + 1) * T)
        self.rmsnorm(self.gain("mix_norm"), self.gnb)

        def epi(oc, bb, bank):
            if oc < 8:
                tb, tmp = self.tmpb[oc % 2], self.tmp[:, oc % 2, :]
                P.op(P.act, lambda: nc.scalar.copy(tmp, bank), reads=[bb], writes=[tb])
                P.dma(P.sp, self.u_out[oc, :, tsl], tmp, reads=[tb])
            else:
                self.out_chunk(self.qkv_out[oc - 8, :, tsl], bank, bb, oc)
        self.proj(self.xn, self.xnb, DC, None, 4096, 256, epi)

    def inproj_odd(self, ti):
        P, nc = self.P, self.nc
        tsl = slice(ti * T, (ti + 1) * T)
        self.rmsnorm(self.gain("mix_norm"), self.gnb)
        rb = self.ropeb
        ang = self.tmp[:, 0, :]
        P.dma(P.sp, ang, self.pos_in[:, tsl], writes=[self.tmpb[0]])
        P.op(P.dve, lambda: nc.vector.tensor_scalar(ang, ang, self.ropec[:, 0:1], None, ALU.mult),
             reads=[self.tmpb[0], self.smallb], writes=[self.tmpb[0]])
        self.sincos(ang, self.tmpb[0], self.tmp[:, 1, :], self.tmpb[1], self.rt[:, 1, :], self.rt[:, 0, :], rb)
        P.op(P.dve, lambda: nc.vector.tensor_scalar(self.rt[:, 1, :], self.rt[:, 1, :], self.ropec[:, 1:2], None, ALU.mult),
             reads=[rb, self.smallb], writes=[rb])

        def epi(oc, bb, bank):
            if oc >= 20:
                self.out_chunk(self.qkv_out[oc, :, tsl], bank, bb, oc)
                return
            gcol = self.qkg[:, 0:1] if oc < 16 else self.qkg[:, 1:2]
            zs, zsb = self.wk[:, 0, :], self.wkb[0]
            sq, sqb = self.wk[:, 1, :], self.wkb[1]
            P.op(P.act, lambda: nc.scalar.copy(zs, bank), reads=[bb], writes=[zsb])
            P.op(P.act, lambda: nc.scalar.activation(sq, bank, AF.Square), reads=[bb], writes=[sqb])
            b2, bank2 = self.ps.get()
            P.pe_deps([sqb, self.cb], [b2])
            ins = nc.tensor.matmul(bank2, self.ones[:], sq, start=True, stop=True)
            P.pe_done(ins, [sqb], [b2])
            rs, rsb = self.wk[:, 2, :], self.wkb[2]
            P.op(P.act, lambda: nc.scalar.activation(rs, bank2, AF.Sqrt, bias=self.epsc[:, 0:1], scale=1.0 / 128),
                 reads=[b2, self.cb], writes=[rsb])
            P.op(P.dve, lambda: nc.vector.reciprocal(rs, rs), reads=[rsb], writes=[rsb])
            P.op(P.dve, lambda: nc.vector.scalar_tensor_tensor(zs, zs, gcol, rs, ALU.mult, ALU.mult),
                 reads=[zsb, rsb, self.smallb], writes=[zsb])
            b3, bank3 = self.ps.get()
            P.pe_deps([zsb, self.smallb], [b3])
            ins = nc.tensor.matmul(bank3, self.perm[:], zs, start=True, stop=True)
            P.pe_done(ins, [zsb], [b3])
            P.op(P.dve, lambda: nc.vector.tensor_tensor(sq, bank3, self.rt[:, 1, :], ALU.mult),
                 reads=[b3, rb], writes=[sqb])
            P.op(P.dve, lambda: nc.vector.tensor_tensor(zs, zs, self.rt[:, 0, :], ALU.mult),
                 reads=[zsb, rb], writes=[zsb])
            sb_, st = self.stgb[oc % 2], self.stg[:, oc % 2, :]
            P.op(P.dve, lambda: nc.vector.tensor_tensor(st, zs, sq, ALU.add), reads=[zsb, sqb], writes=[sb_])
            P.dma(P.sp, self.qkv_out[oc, :, tsl], st, reads=[sb_])
        self.proj(self.xn, self.xnb, DC, None, 3072, 256, epi)

    def gain(self, nm):
        return self.gn[:, self.gidx[nm], :]

    def build(self, plan):
        nc, P = self.nc, self.P
        NTOK = self.NTOK
        ops = plan["ops"]
        first = plan["first"]
        W = {}
        self.W = W
        if first:
            self.x_tok = self.din("x_tok", [NTOK, D])
        else:
            self.xT_in = self.din("xT_in", [DC, 128, NTOK])
        self.cst = self.din("cst", [128, 384])
        gnames = []

        def need(nm, shape, dt=F32):
            W[nm] = self.din(nm, shape, dt)
        if "ffn1" in ops:
            need("ffn1_w_gu", [D, 2 * DFF]); need("ffn1_w_down", [DFF, D]); gnames.append("ffn1_norm")
        if "ffn2" in ops:
            need("ffn2_w_gu", [D, 2 * DFF]); need("ffn2_w_down", [DFF, D]); gnames.append("ffn2_norm")
        if "cross" in ops:
            self.memin = self.din("mem", [2 * NMEM, D])
            need("cross_w_q", [D, D]); need("cross_w_kv", [D, 2 * D]); need("cross_w_o", [D, D])
            gnames += ["cross_norm", "mem_norm"]
        if "mixout_even" in ops or "mixout_odd" in ops:
            need("mix_w_out", [D, D])
            self.yb_in = self.din("yb_in", [8 if "mixout_even" in ops else DC, 128, NTOK], BF16)
        if "mixout_even" in ops:
            need("s5_glu_w", [1024, 1024])
            self.ys5_in = self.din("ys5_in", [8, 128, NTOK])
        if "inproj_even" in ops:
            need("mix_w_in", [D, 4096]); gnames.append("mix_norm")
            self.u_out = nc.dram_tensor("u_out", [8, 128, NTOK], F32, kind="ExternalOutput").ap()
            self.qkv_out = nc.dram_tensor("qkv_out", [24, 128, NTOK], BF16, kind="ExternalOutput").ap()
        if "inproj_odd" in ops:
            need("mix_w_in", [D, 3072]); gnames.append("mix_norm")
            self.pos_in = self.din("pos_in", [128, NTOK])
            self.qkv_out = nc.dram_tensor("qkv_out", [24, 128, NTOK], BF16, kind="ExternalOutput").ap()
        if "final" in ops:
            gnames.append("final_norm")
            self.y_tok = nc.dram_tensor("y_tok", [NTOK, D], F32, kind="ExternalOutput").ap()
        else:
            self.xT_out = nc.dram_tensor("xT_out", [DC, 128, NTOK], F32, kind="ExternalOutput").ap()
        self.small_in = self.din("small", [128, 64])
        self.perm_in = self.din("perm", [128, 128])
        for g in gnames:
            need(g, [D])
        self.gidx = {g: i for i, g in enumerate(gnames)}

        from contextlib import ExitStack
        with ExitStack() as es:
            def sb(name, shape, dt):
                return es.enter_context(nc.sbuf_tensor(name, shape, dt))
            self.pst = es.enter_context(nc.psum_tensor("pst", [128, 4096], F32))
            self.ps = PSum(P, self.pst)
            self.cs = sb("cs", [128, 384], F32)
            self.cb = Buf("cst")
            self.ident = self.cs[:, 0:128]
            self.ones = self.cs[:, 128:256]
            self.epsc = self.cs[:, 256:257]
            self.halfpi = self.cs[:, 257:258]
            P.dma(P.sp, self.cs[:, :], self.cst[:, :], writes=[self.cb])
            self.small = sb("small_sb", [128, 64], F32)
            self.smallb = Buf("small")
            P.dma(P.sp, self.small[:, :], self.small_in[:, :], writes=[self.smallb])
            self.glub = self.small[:, 0:8]
            self.ropec = self.small[:, 8:11]
            self.qkg = self.small[:, 11:13]
            self.perm = sb("perm_sb", [128, 128], F32)
            P.dma(P.sp, self.perm[:, :], self.perm_in[:, :], writes=[self.smallb])
            self.gn = sb("gn", [128, max(1, len(gnames)), DC], F32)
            self.gnb = Buf("gn")
            for g in gnames:
                P.dma(P.sp, self.gn[:, self.gidx[g], :], W[g].rearrange("(c p) -> p c", p=128),
                      writes=[self.gnb], slow=True)

            self.x = sb("x", [128, DC, T], F32)
            self.xb = [Buf(f"x{c}") for c in range(DC)]
            self.xn = sb("xn", [128, DC, T], BF16)
            self.xnb = [Buf(f"xn{c}") for c in range(DC)]
            self.sc = sb("sc", [128, FC, T], BF16)
            self.scb = [Buf(f"sc{c}") for c in range(FC)]
            self.sq = sb("sq", [128, 2, T], F32)
            self.sqb = [Buf(f"sq{c}") for c in range(2)]
            self.tmp = sb("tmp", [128, 2, T], F32)
            self.tmpb = [Buf(f"tmp{c}") for c in range(2)]
            self.rstd = sb("rstd", [128, T], F32)
            self.rsb = Buf("rstd")
            self.tok = sb("tok", [128, D], F32)
            self.tokb = Buf("tok")
            self.stg = sb("stg", [128, 2, T], BF16)
            self.stgb = [Buf(f"stg{c}") for c in range(2)]
            if "cross" in ops:
                self.memK = sb("memK", [128, DC, 2 * NMEM], BF16)
                self.memKb = Buf("memK")
                self.memV = sb("memV", [128, 4, D], BF16)
                self.memVb = Buf("memV")
            if "inproj_odd" in ops:
                self.rt = sb("rt", [128, 2, T], F32)
                self.ropeb = Buf("rt")
                self.wk = sb("wk", [128, 3, T], F32)
                self.wkb = [Buf(f"wk{c}") for c in range(3)]
            self.onesb = sb("onesb", [128, 128], BF16)
            self.cb2 = Buf("onesb")
            P.op(P.dve, lambda: nc.vector.tensor_copy(self.onesb[:], self.ones), reads=[self.cb], writes=[self.cb2])
            NSLOT, SLOT = 4, 6144
            self.wrt = sb("wrt", [128, NSLOT, SLOT], BF16)
            self.wr = WRing(P, self.wrt, NSLOT, SLOT)

            if "cross" in ops:
                self.enq_memkv(W["cross_w_kv"])
                self.memkv(self.gain("mem_norm"), self.gnb, W["cross_w_kv"])
            for ti in range(self.ntile):
                for o in ops:
                    if o == "mixout_even": self.enq_mixout(True)
                    elif o == "mixout_odd": self.enq_mixout(False)
                    elif o == "cross": self.enq_cross(W["cross_w_q"], W["cross_w_o"])
                    elif o == "ffn1": self.enq_ffn(W["ffn1_w_gu"], W["ffn1_w_down"])
                    elif o == "ffn2": self.enq_ffn(W["ffn2_w_gu"], W["ffn2_w_down"])
                    elif o == "inproj_even": self.enq_inproj(True)
                    elif o == "inproj_odd": self.enq_inproj(False)
                if first:
                    self.load_x_tok(ti)
                else:
                    for c in range(DC):
                        P.dma(P.sp, self.x[:, c, :], self.xT_in[c, :, ti * T:(ti + 1) * T], writes=[self.xb[c]])
                for o in ops:
                    if o == "mixout_even": self.mixout(ti, True)
                    elif o == "mixout_odd": self.mixout(ti, False)
                    elif o == "cross": self.cross(ti, self.gain("cross_norm"), self.gnb)
                    elif o == "ffn1": self.ffn(self.gain("ffn1_norm"), self.gnb, None, None)
                    elif o == "ffn2": self.ffn(self.gain("ffn2_norm"), self.gnb, None, None)
                    elif o == "inproj_even": self.inproj_even(ti)
                    elif o == "inproj_odd": self.inproj_odd(ti)
                    elif o == "final": self.store_y_tok(ti, self.gain("final_norm"), self.gnb)
                if "final" not in ops:
                    for c in range(DC):
                        P.dma(P.sp, self.xT_out[c, :, ti * T:(ti + 1) * T], self.x[:, c, :], reads=[self.xb[c]])
            P.barrier()
        return nc


class KernMO:
    def __init__(self, LP, LS):
        self.Ls = [LP, LS]
        self.nc = bass.Bass("TRN2", target_bir_lowering=False)
        self.P = Prog(self.nc)

    def build(self):
        nc, P = self.nc, self.P
        LM = max(self.Ls)
        qi, ki, vi, oo = [], [], [], []
        for s, L in enumerate(self.Ls):
            qi.append(nc.dram_tensor(f"q{s}", [2, 128, L], BF16, kind="ExternalInput").ap())
            ki.append(nc.dram_tensor(f"k{s}", [128, L], BF16, kind="ExternalInput").ap())
            vi.append(nc.dram_tensor(f"v{s}", [128, L // 128, 128], BF16, kind="ExternalInput").ap())
            oo.append(nc.dram_tensor(f"o{s}", [2, 128, L], BF16, kind="ExternalOutput").ap())
        from contextlib import ExitStack
        with ExitStack() as es:
            def sb(name, shape, dt):
                return es.enter_context(nc.sbuf_tensor(name, shape, dt))
            pst = es.enter_context(nc.psum_tensor("pst", [128, 4096], F32))
            ps = PSum(P, pst, banks=[4, 5, 6, 7])
            kT = sb("kT", [128, LM], BF16); kTb = Buf("kT")
            V = sb("V", [128, LM // 128, 128], BF16); Vb = Buf("V")
            qt_ = sb("qt", [128, 2, 2, T], BF16); qb = [Buf("q0"), Buf("q1")]
            NE = 6
            E = sb("E", [128, NE, T], BF16); Eb = [Buf(f"E{i}") for i in range(NE)]
            onesb = sb("onesb", [128, 128], F32); ob = Buf("ones")
            ES = sb("ES", [128, 2, 2, T], F32); esb = [[Buf(f"es{a}{b}") for b in range(2)] for a in range(2)]
            rz = sb("rz", [128, 2, T], F32); rzb = [Buf("rz0"), Buf("rz1")]
            ost = sb("ost", [128, 2, T], BF16); ostb = [Buf("os0"), Buf("os1")]
            P.op(P.dve, lambda: nc.vector.memset(onesb[:], 1.0), writes=[ob])
            scale = 128.0 ** -0.5
            ek = 0
            for s, L in enumerate(self.Ls):
                P.dma(P.sp, kT[:, 0:L], ki[s][:, :], writes=[kTb])
                P.dma(P.sp, V[:, 0:L // 128, :], vi[s][:, :, :], writes=[Vb])
                nkb = L // 128
                for qt in range(L // T):
                    qs = qt % 2
                    P.dma(P.sp, qt_[:, qs, :, :], qi[s][:, :, qt * T:(qt + 1) * T].rearrange("h p t -> p h t"),
                          writes=[qb[qs]])
                    items = [(kb, h) for kb in range(nkb) for h in range(2)]
                    sc_banks = {}

                    def emit_S(idx):
                        kb, h = items[idx]
                        b, bank = ps.get()
                        P.pe_deps([kTb, qb[qs]], [b])
                        ins = nc.tensor.matmul(bank, kT[:, kb * 128:(kb + 1) * 128], qt_[:, qs, h, :], start=True, stop=True)
                        P.pe_done(ins, [kTb, qb[qs]], [b])
                        sc_banks[idx] = (b, bank)
                    SK = 2
                    for i in range(min(SK, len(items))):
                        emit_S(i)
                    for idx, (kb, h) in enumerate(items):
                        if idx + SK < len(items):
                            emit_S(idx + SK)
                        b, bank = sc_banks.pop(idx)
                        e = ek % NE
                        ek += 1
                        P.op(P.act, lambda: nc.scalar.activation(E[:, e, :], bank, AF.Exp, scale=scale),
                             reads=[b], writes=[Eb[e]])
                        ab, abank = ps.fixed(2 * h)
                        first, last = kb == 0, kb == nkb - 1
                        P.pe_deps([Eb[e], Vb], [ab] if first else [])
                        ins = nc.tensor.matmul(abank, V[:, kb, :], E[:, e, :], start=first, stop=last)
                        P.pe_done(ins, [Eb[e], Vb], [ab] if last else [])
                        w = 1 if kb % 3 == 2 else 0
                        eng, engh = (P.pool, nc.gpsimd) if w else (P.dve, nc.vector)
                        if kb == (2 if w else 0):
                            P.op(eng, lambda: engh.tensor_copy(ES[:, h, w, :], E[:, e, :]), reads=[Eb[e]], writes=[esb[h][w]])
                        else:
                            P.op(eng, lambda: engh.tensor_tensor(ES[:, h, w, :], ES[:, h, w, :], E[:, e, :], ALU.add),
                                 reads=[Eb[e], esb[h][w]], writes=[esb[h][w]])
                    for h in range(2):
                        ab, abank = ps.fixed(2 * h)
                        zb, zbank = ps.fixed(2 * h + 1)
                        P.pe_deps([esb[h][0], esb[h][1], ob], [zb])
                        nc.tensor.matmul(zbank, onesb[:], ES[:, h, 0, :], start=True, stop=False)
                        ins = nc.tensor.matmul(zbank, onesb[:], ES[:, h, 1, :], start=False, stop=True)
                        P.pe_done(ins, [esb[h][0], esb[h][1]], [zb])
                        P.op(P.dve, lambda: nc.vector.reciprocal(rz[:, h, :], zbank), reads=[zb], writes=[rzb[h]])
                        P.op(P.dve, lambda: nc.vector.tensor_tensor(ost[:, h, :], abank, rz[:, h, :], ALU.mult),
                             reads=[ab, rzb[h]], writes=[ostb[h]])
                        P.dma(P.sp, oo[s][h, :, qt * T:(qt + 1) * T], ost[:, h, :], reads=[ostb[h]])
            P.barrier()
        return nc


class KernMD:
    def __init__(self, LP, LS, lambda_init):
        self.Ls = [LP, LS]
        self.li = float(lambda_init)
        self.nc = bass.Bass("TRN2", target_bir_lowering=False)
        self.P = Prog(self.nc)

    def build(self):
        nc, P = self.nc, self.P
        LM = max(self.Ls)
        KR = 68
        ka, qa, vi, oo = [], [], [], []
        for s, L in enumerate(self.Ls):
            ka.append(nc.dram_tensor(f"ka{s}", [2, KR, L], BF16, kind="ExternalInput").ap())
            qa.append(nc.dram_tensor(f"qa{s}", [2, 3, KR, L], BF16, kind="ExternalInput").ap())
            vi.append(nc.dram_tensor(f"v{s}", [128, L // 128, 128], BF16, kind="ExternalInput").ap())
            oo.append(nc.dram_tensor(f"o{s}", [128, L], BF16, kind="ExternalOutput").ap())
        dist_in = nc.dram_tensor("dist", [128, 4, T], F32, kind="ExternalInput").ap()
        par_in = nc.dram_tensor("par", [128, 4 * 64 + 8], F32, kind="ExternalInput").ap()
        from contextlib import ExitStack
        with ExitStack() as es:
            def sb(name, shape, dt):
                return es.enter_context(nc.sbuf_tensor(name, shape, dt))
            pst = es.enter_context(nc.psum_tensor("pst", [128, 4096], F32))
            ps = PSum(P, pst, banks=[4, 5, 6, 7])
            KA = sb("KA", [KR, 2, LM], BF16); KAb = Buf("KA")
            V = sb("V", [128, LM // 128, 128], BF16); Vb = Buf("V")
            QT = sb("QT", [KR, 2, 2, 3, T], BF16); qb = [Buf("q0"), Buf("q1")]
            NE = 6
            E = sb("E", [128, NE, T], BF16); Eb = [Buf(f"E{i}") for i in range(NE)]
            ES = sb("ES", [128, 2, 2, T], F32); esb = [[Buf(f"es{a}{b}") for b in range(2)] for a in range(2)]
            DIST = sb("dist_sb", [128, 4, T], F32); Db = Buf("dist")
            par = sb("par_sb", [128, 4 * 64 + 8], F32); pb = Buf("par")
            onesb = sb("onesb", [128, 128], BF16); ob = Buf("ones")
            onesf = sb("onesf", [128, 128], F32)
            bt = sb("bt", [128, 2, T], F32); btb = [Buf("bt0"), Buf("bt1")]
            wk = sb("wk", [128, 4, T], F32); wkb = [Buf(f"wk{i}") for i in range(4)]
            ost = sb("ost", [128, 2, T], BF16); ostb = [Buf("os0"), Buf("os1")]
            col = sb("col", [128, 16], F32); cb_ = Buf("col")
            P.op(P.dve, lambda: nc.vector.memset(onesb[:], 1.0), writes=[ob])
            P.op(P.dve, lambda: nc.vector.memset(onesf[:], 1.0), writes=[ob])
            P.dma(P.sp, DIST[:, :, :], dist_in[:, :, :], writes=[Db])
            P.dma(P.sp, par[:, :], par_in[:, :], writes=[pb])
            pr = wk[:, 0, 0:128]
            P.op(P.dve, lambda: nc.vector.tensor_tensor(pr[:, 0:64], par[:, 0:64], par[:, 64:128], ALU.mult),
                 reads=[pb], writes=[wkb[0]])
            P.op(P.dve, lambda: nc.vector.tensor_tensor(pr[:, 64:128], par[:, 128:192], par[:, 192:256], ALU.mult),
                 reads=[pb], writes=[wkb[0]])
            P.op(P.dve, lambda: nc.vector.reduce_sum(col[:, 0:1], pr[:, 0:64], mybir.AxisListType.X),
                 reads=[wkb[0]], writes=[cb_])
            P.op(P.dve, lambda: nc.vector.reduce_sum(col[:, 1:2], pr[:, 64:128], mybir.AxisListType.X),
                 reads=[wkb[0]], writes=[cb_])
            P.op(P.act, lambda: nc.scalar.activation(col[:, 2:4], col[:, 0:2], AF.Exp), reads=[cb_], writes=[cb_])
            P.op(P.dve, lambda: nc.vector.tensor_tensor(col[:, 4:5], col[:, 3:4], col[:, 2:3], ALU.subtract),
                 reads=[cb_], writes=[cb_])
            P.op(P.dve, lambda: nc.vector.tensor_scalar(col[:, 5:6], col[:, 4:5], -self.li, None, ALU.add),
                 reads=[cb_], writes=[cb_])
            P.op(P.dve, lambda: nc.vector.tensor_scalar(col[:, 6:7], par[:, 257:258], 1.0 - self.li, None, ALU.mult),
                 reads=[pb, cb_], writes=[cb_])
            negsig = par[:, 256:257]
            neglam = col[:, 5:6]
            gsub = col[:, 6:7]
            epsc = par[:, 258:259]
            scale = 64.0 ** -0.5
            ek = 0
            for s, L in enumerate(self.Ls):
                for m in range(2):
                    P.dma(P.sp, KA[:, m, 0:L], ka[s][m, :, :], writes=[KAb])
                P.dma(P.sp, V[:, 0:L // 128, :], vi[s][:, :, :], writes=[Vb])
                nkb = L // 128
                for qt in range(L // T):
                    qs = qt % 2
                    for m in range(2):
                        P.dma(P.sp, QT[:, qs, m, :, :],
                              qa[s][m, :, :, qt * T:(qt + 1) * T].rearrange("v r t -> r v t"), writes=[qb[qs]])
                    items = [(kb, m) for kb in range(nkb) for m in range(2)]
                    sc_banks = {}

                    def emit_S(idx):
                        kb, m = items[idx]
                        rel = kb - 4 * qt
                        var = 0 if rel < 0 else (1 if rel > 3 else 2)
                        b, bank = ps.get()
                        P.pe_deps([KAb, qb[qs]], [b])
                        ins = nc.tensor.matmul(bank, KA[:, m, kb * 128:(kb + 1) * 128], QT[:, qs, m, var, :],
                                               start=True, stop=True)
                        P.pe_done(ins, [KAb, qb[qs]], [b])
                        sc_banks[idx] = (b, bank)
                    SK = 2
                    for i in range(min(SK, len(items))):
                        emit_S(i)
                    for idx, (kb, m) in enumerate(items):
                        if idx + SK < len(items):
                            emit_S(idx + SK)
                        b, bank = sc_banks.pop(idx)
                        rel = kb - 4 * qt
                        e = ek % NE
                        ek += 1
                        if 0 <= rel <= 3:
                            tb, tt = btb[ek % 2], bt[:, ek % 2, :]
                            P.op(P.dve, lambda: nc.vector.scalar_tensor_tensor(tt, DIST[:, rel, :], negsig, bank,
                                                                               ALU.mult, ALU.add),
                                 reads=[b, Db, pb], writes=[tb])
                            P.op(P.act, lambda: nc.scalar.activation(E[:, e, :], tt, AF.Exp, scale=scale),
                                 reads=[tb], writes=[Eb[e]])
                        else:
                            P.op(P.act, lambda: nc.scalar.activation(E[:, e, :], bank, AF.Exp, scale=scale),
                                 reads=[b], writes=[Eb[e]])
                        ab, abank = ps.fixed(2 * m)
                        first, last = kb == 0, kb == nkb - 1
                        P.pe_deps([Eb[e], Vb], [ab] if first else [])
                        ins = nc.tensor.matmul(abank, V[:, kb, :], E[:, e, :], start=first, stop=last)
                        P.pe_done(ins, [Eb[e], Vb], [ab] if last else [])
                        w = 1 if kb % 3 == 2 else 0
                        eng, engh = (P.pool, nc.gpsimd) if w else (P.dve, nc.vector)
                        if kb == (2 if w else 0):
                            P.op(eng, lambda: engh.tensor_copy(ES[:, m, w, :], E[:, e, :]), reads=[Eb[e]], writes=[esb[m][w]])
                        else:
                            P.op(eng, lambda: engh.tensor_tensor(ES[:, m, w, :], ES[:, m, w, :], E[:, e, :], ALU.add),
                                 reads=[Eb[e], esb[m][w]], writes=[esb[m][w]])
                    for m in range(2):
                        zb, zbank = ps.fixed(2 * m + 1)
                        P.pe_deps([esb[m][0], esb[m][1], ob], [zb])
                        nc.tensor.matmul(zbank, onesf[:], ES[:, m, 0, :], start=True, stop=False)
                        ins = nc.tensor.matmul(zbank, onesf[:], ES[:, m, 1, :], start=False, stop=True)
                        P.pe_done(ins, [esb[m][0], esb[m][1]], [zb])
                    a0b, a0 = ps.fixed(0); z0b, z0 = ps.fixed(1); a1b, a1 = ps.fixed(2); z1b, z1 = ps.fixed(3)
                    P.op(P.dve, lambda: nc.vector.reciprocal(wk[:, 0, :], z0), reads=[z0b], writes=[wkb[0]])
                    P.op(P.dve, lambda: nc.vector.reciprocal(wk[:, 1, :], z1), reads=[z1b], writes=[wkb[1]])
                    P.op(P.dve, lambda: nc.vector.tensor_tensor(wk[:, 0, :], a0, wk[:, 0, :], ALU.mult),
                         reads=[a0b, wkb[0]], writes=[wkb[0]])
                    P.op(P.dve, lambda: nc.vector.scalar_tensor_tensor(wk[:, 1, :], a1, neglam, wk[:, 1, :],
                                                                       ALU.mult, ALU.mult),
                         reads=[a1b, wkb[1], cb_], writes=[wkb[1]])
                    P.op(P.dve, lambda: nc.vector.tensor_tensor(wk[:, 0, :], wk[:, 0, :], wk[:, 1, :], ALU.add),
                         reads=[wkb[0], wkb[1]], writes=[wkb[0]])
                    P.op(P.act, lambda: nc.scalar.activation(wk[:, 2, :], wk[:, 0, :], AF.Square),
                         reads=[wkb[0]], writes=[wkb[2]])
                    b2, bank2 = ps.get()
                    P.pe_deps([wkb[2], ob], [b2])
                    ins = nc.tensor.matmul(bank2, onesf[:], wk[:, 2, :], start=True, stop=True)
                    P.pe_done(ins, [wkb[2]], [b2])
                    P.op(P.act, lambda: nc.scalar.activation(wk[:, 3, :], bank2, AF.Sqrt, bias=epsc, scale=1.0 / 128),
                         reads=[b2, pb], writes=[wkb[3]])
                    P.op(P.dve, lambda: nc.vector.reciprocal(wk[:, 3, :], wk[:, 3, :]), reads=[wkb[3]], writes=[wkb[3]])
                    os_, osb = ost[:, qt % 2, :], ostb[qt % 2]
                    P.op(P.dve, lambda: nc.vector.scalar_tensor_tensor(os_, wk[:, 0, :], gsub, wk[:, 3, :],
                                                                       ALU.mult, ALU.mult),
                         reads=[wkb[0], wkb[3], cb_], writes=[osb])
                    P.dma(P.sp, oo[s][:, qt * T:(qt + 1) * T], os_, reads=[osb])
            P.barrier()
        return nc


class KernMS:
    def __init__(self, LP, LS):
        self.Ls = [LP, LS]
        self.nc = bass.Bass("TRN2", target_bir_lowering=False)
        self.P = Prog(self.nc)

    def build(self):
        nc, P = self.nc, self.P
        LM = max(self.Ls)
        ui, yo = [], []
        for s, L in enumerate(self.Ls):
            ui.append(nc.dram_tensor(f"u{s}", [128, L], F32, kind="ExternalInput").ap())
            yo.append(nc.dram_tensor(f"y{s}", [128, L], F32, kind="ExternalOutput").ap())
        spar_in = nc.dram_tensor("spar", [128, 32], F32, kind="ExternalInput").ap()
        bc_in = nc.dram_tensor("bc", [128, 4, 8, 16], F32, kind="ExternalInput").ap()
        iota_in = nc.dram_tensor("iota", [128, T], F32, kind="ExternalInput").ap()
        id_in = nc.dram_tensor("ident", [128, 128], F32, kind="ExternalInput").ap()
        from contextlib import ExitStack
        with ExitStack() as es:
            def sb(name, shape, dt):
                return es.enter_context(nc.sbuf_tensor(name, shape, dt))
            pst = es.enter_context(nc.psum_tensor("pst", [128, 4096], F32))
            ps = PSum(P, pst, banks=[2, 3, 4, 5, 6, 7])
            spar = sb("spar_sb", [128, 32], F32); spb = Buf("spar")
            bc = sb("bc_sb", [128, 4, 8, 16], F32); bcb = Buf("bc")
            iota = sb("iota_sb", [128, T], F32); iob = Buf("iota")
            ident = sb("ident_sb", [128, 128], F32); idb = Buf("ident")
            P.dma(P.sp, spar[:, :], spar_in[:, :], writes=[spb])
            P.dma(P.sp, bc[:, :, :, :], bc_in[:, :, :, :], writes=[bcb])
            P.dma(P.sp, iota[:, :], iota_in[:, :], writes=[iob])
            P.dma(P.sp, ident[:, :], id_in[:, :], writes=[idb])
            cw = sb("cw", [128, 24, 8], F32); cwb = Buf("cw")
            TAB = sb("TAB", [128, 5, 8, T], F32); tabb = Buf("tab")
            LB = sb("LB", [128, 4, 8, 128], F32); lbb = Buf("LB")
            BM = sb("BM", [128, 128], F32); bmb = Buf("BM")
            yst = sb("yst", [128, 2, T], F32); ystb = [Buf("ys0"), Buf("ys1")]
            wkt2 = sb("wkt", [128, 2, 10, T], F32); wb2 = [[Buf(f"wk{a}_{i}") for i in range(10)] for a in range(2)]
            wkt = wkt2[:, 0, :, :]; wb = wb2[0]
            ut = sb("ut", [128, 2, T], F32); utb = [Buf("u0"), Buf("u1")]
            carry = sb("carry", [128, 8, 2], F32); cyb = [Buf(f"cy{i}") for i in range(8)]
            halfpi = spar[:, 24:25]
            MAGIC = 12582912.0
            PI = float(np.pi)

            def sincos(ang, angb, wk_, wkb_, sin_out, cos_out, outb):
                P.op(P.dve, lambda: nc.vector.tensor_scalar(wk_, ang, 1.0 / (2 * PI), MAGIC, ALU.mult, ALU.add),
                     reads=[angb], writes=[wkb_])
                P.op(P.dve, lambda: nc.vector.tensor_scalar(wk_, wk_, MAGIC, -2 * PI, ALU.subtract, ALU.mult),
                     reads=[wkb_], writes=[wkb_])
                P.op(P.dve, lambda: nc.vector.tensor_tensor(ang, ang, wk_, ALU.add), reads=[angb, wkb_], writes=[angb])
                P.op(P.act, lambda: nc.scalar.activation(sin_out, ang, AF.Sin), reads=[angb], writes=[outb])
                P.op(P.act, lambda: nc.scalar.activation(wk_, ang, AF.Abs), reads=[angb], writes=[wkb_])
                P.op(P.act, lambda: nc.scalar.activation(cos_out, wk_, AF.Sin, bias=halfpi, scale=-1.0),
                     reads=[wkb_, spb], writes=[outb])

            C = lambda i: cw[:, i, :]
            lamre, lamim, logdt = spar[:, 0:8], spar[:, 8:16], spar[:, 16:24]

            def cop(fn, reads=()):
                P.op(P.dve, fn, reads=[cwb, spb], writes=[cwb])
            LR, DT, LDT, RR, TH, SN, CS, ABRE, ABIM, NR, DEN, T1, T2, FRE, FIM, NFRE, W1, W2 = range(18)
            cop(lambda: nc.vector.tensor_scalar_min(C(LR), lamre, -1e-4))
            P.op(P.act, lambda: nc.scalar.activation(C(DT), logdt, AF.Exp), reads=[spb], writes=[cwb])
            cop(lambda: nc.vector.tensor_tensor(C(LDT), C(LR), C(DT), ALU.mult))
            P.op(P.act, lambda: nc.scalar.activation(C(RR), C(LDT), AF.Exp), reads=[cwb], writes=[cwb])
            cop(lambda: nc.vector.tensor_tensor(C(TH), lamim, C(DT), ALU.mult))
            cop(lambda: nc.vector.tensor_copy(C(W1), C(TH)))
            sincos(C(W1), cwb, C(W2), cwb, C(SN), C(CS), cwb)
            cop(lambda: nc.vector.tensor_tensor(C(ABRE), C(RR), C(CS), ALU.mult))
            cop(lambda: nc.vector.tensor_tensor(C(ABIM), C(RR), C(SN), ALU.mult))
            cop(lambda: nc.vector.tensor_scalar(C(NR), C(ABRE), -1.0, None, ALU.add))
            cop(lambda: nc.vector.tensor_tensor(C(DEN), C(LR), C(LR), ALU.mult))
            cop(lambda: nc.vector.tensor_tensor(C(T1), lamim, lamim, ALU.mult))
            cop(lambda: nc.vector.tensor_tensor(C(DEN), C(DEN), C(T1), ALU.add))
            cop(lambda: nc.vector.reciprocal(C(DEN), C(DEN)))
            cop(lambda: nc.vector.tensor_tensor(C(T1), C(NR), C(LR), ALU.mult))
            cop(lambda: nc.vector.tensor_tensor(C(T2), C(ABIM), lamim, ALU.mult))
            cop(lambda: nc.vector.tensor_tensor(C(T1), C(T1), C(T2), ALU.add))
            cop(lambda: nc.vector.tensor_tensor(C(FRE), C(T1), C(DEN), ALU.mult))
            cop(lambda: nc.vector.tensor_tensor(C(T1), C(ABIM), C(LR), ALU.mult))
            cop(lambda: nc.vector.tensor_tensor(C(T2), C(NR), lamim, ALU.mult))
            cop(lambda: nc.vector.tensor_tensor(C(T1), C(T1), C(T2), ALU.subtract))
            cop(lambda: nc.vector.tensor_tensor(C(FIM), C(T1), C(DEN), ALU.mult))
            cop(lambda: nc.vector.tensor_scalar(C(NFRE), C(FRE), -1.0, None, ALU.mult))
            for u_ in range(8):
                gp = u_ % 4
                ang, angb = wkt[:, 0, :], wb[0]
                w2, w2b = wkt[:, 1, :], wb[1]
                P.op(P.dve, lambda: nc.vector.tensor_scalar(ang, iota[:, :], cw[:, TH, u_:u_ + 1], None, ALU.mult),
                     reads=[iob, cwb], writes=[angb])
                sincos(ang, angb, w2, w2b, TAB[:, 3, u_, :], TAB[:, 2, u_, :], tabb)
                cc, ss = TAB[:, 2, u_, :], TAB[:, 3, u_, :]
                P.op(P.dve, lambda: nc.vector.tensor_scalar(w2, cc, cw[:, FRE, u_:u_ + 1], None, ALU.mult),
                     reads=[tabb, cwb], writes=[w2b])
                P.op(P.dve, lambda: nc.vector.scalar_tensor_tensor(TAB[:, 0, u_, :], ss, cw[:, FIM, u_:u_ + 1], w2,
                                                                   ALU.mult, ALU.add),
                     reads=[tabb, cwb, w2b], writes=[tabb])
                P.op(P.dve, lambda: nc.vector.tensor_scalar(w2, cc, cw[:, FIM, u_:u_ + 1], None, ALU.mult),
                     reads=[tabb, cwb], writes=[w2b])
                P.op(P.dve, lambda: nc.vector.scalar_tensor_tensor(TAB[:, 1, u_, :], ss, cw[:, NFRE, u_:u_ + 1], w2,
                                                                   ALU.mult, ALU.add),
                     reads=[tabb, cwb, w2b], writes=[tabb])
                P.op(P.dve, lambda: nc.vector.tensor_scalar(TAB[:, 4, u_, :], iota[:, :], 0.0, cw[:, RR, u_:u_ + 1],
                                                            ALU.mult, ALU.add),
                     reads=[iob, cwb], writes=[tabb])
                for k in range(4):
                    P.op(P.dve, lambda: nc.vector.memset(BM[:, :], 0.0), writes=[bmb])
                    for gl in range(2):
                        c0 = 16 * (2 * gp + gl)
                        if k == 3:
                            P.op(P.dve, lambda: nc.vector.tensor_scalar(BM[64 * gl:64 * gl + 64, c0:c0 + 16],
                                                                        bc[64 * gl:64 * gl + 64, k, u_, :], -1.0, None, ALU.mult),
                                 reads=[bcb], writes=[bmb])
                        else:
                            P.op(P.dve, lambda: nc.vector.tensor_copy(BM[64 * gl:64 * gl + 64, c0:c0 + 16],
                                                                      bc[64 * gl:64 * gl + 64, k, u_, :]),
                                 reads=[bcb], writes=[bmb])
                    if k < 2:
                        b, bank = ps.get()
                        P.pe_deps([bmb, idb], [b])
                        ins = nc.tensor.transpose(bank[:, 0:128], BM[:, :], ident[:, :])
                        P.pe_done(ins, [bmb], [b])
                        P.op(P.act, lambda: nc.scalar.copy(LB[:, k, u_, :], bank[:, 0:128]), reads=[b], writes=[lbb])
                    else:
                        P.op(P.act, lambda: nc.scalar.copy(LB[:, k, u_, :], BM[:, :]), reads=[bmb], writes=[lbb])
            dcol = spar[:, 25:26]
            yob = [[Buf(f"yo{s}_{i}") for i in range(L // T)] for s, L in enumerate(self.Ls)]
            uk = 0
            for s, L in enumerate(self.Ls):
                nt = L // T
                for u_ in range(8):
                    P.op(P.dve, lambda: nc.vector.memset(carry[:, u_, :], 0.0), writes=[cyb[u_]])
                for d in range(2):
                    order = list(range(nt)) if d == 0 else list(range(nt - 1, -1, -1))
                    rv = (lambda ap: ap) if d == 0 else (lambda ap: ap[:, ::-1])
                    last = T - 1 if d == 0 else 0
                    for ti in order:
                        tsl = slice(ti * T, (ti + 1) * T)
                        us, usb = ut[:, uk % 2, :], utb[uk % 2]
                        uk += 1
                        P.dma(P.sp, us, ui[s][:, tsl], writes=[usb])
                        ys_, ysb = yst[:, uk % 2, :], ystb[uk % 2]
                        if d == 0:
                            P.op(P.dve, lambda: nc.vector.tensor_scalar(ys_, us, dcol, None, ALU.mult),
                                 reads=[usb, spb], writes=[ysb])
                        else:
                            P.dma(P.sp, ys_, yo[s][:, tsl], reads=[yob[s][ti]], writes=[ysb])
                        yb_, ybank = ps.fixed(uk % 2)
                        for gp in range(4):
                            u_ = d * 4 + gp
                            b1, bank1 = ps.get()
                            b2, bank2 = ps.get()
                            P.pe_deps([lbb, usb], [b1])
                            ins = nc.tensor.matmul(bank1, LB[:, 0, u_, :], us, start=True, stop=True)
                            P.pe_done(ins, [lbb, usb], [b1])
                            P.pe_deps([lbb, usb], [b2])
                            ins = nc.tensor.matmul(bank2, LB[:, 1, u_, :], us, start=True, stop=True)
                            P.pe_done(ins, [lbb, usb], [b2])
                            TcI, TsI, TcO, TsO, Rt = (TAB[:, k, u_, :] for k in range(5))
                            t = [wkt2[:, gp % 2, k, :] for k in range(10)]
                            wb = wb2[gp % 2]
                            P.op(P.dve, lambda: nc.vector.tensor_tensor(t[0], rv(bank1), TcI, ALU.mult), reads=[b1, tabb], writes=[wb[0]])
                            P.op(P.dve, lambda: nc.vector.tensor_tensor(t[1], rv(bank2), TsI, ALU.mult), reads=[b2, tabb], writes=[wb[1]])
                            P.op(P.dve, lambda: nc.vector.tensor_tensor(t[2], rv(bank1), TsI, ALU.mult), reads=[b1, tabb], writes=[wb[2]])
                            P.op(P.dve, lambda: nc.vector.tensor_tensor(t[3], rv(bank2), TcI, ALU.mult), reads=[b2, tabb], writes=[wb[3]])
                            P.op(P.pool, lambda: nc.gpsimd.tensor_tensor(t[4], t[0], t[1], ALU.subtract), reads=[wb[0], wb[1]], writes=[wb[4]])
                            P.op(P.pool, lambda: nc.gpsimd.tensor_tensor(t[5], t[2], t[3], ALU.add), reads=[wb[2], wb[3]], writes=[wb[5]])
                            P.op(P.dve, lambda: nc.vector.tensor_tensor_scan(t[6], Rt, t[4], carry[:, u_, 0:1], ALU.mult, ALU.add),
                                 reads=[wb[4], tabb, cyb[u_]], writes=[wb[6]])
                            P.op(P.dve, lambda: nc.vector.tensor_tensor_scan(t[7], Rt, t[5], carry[:, u_, 1:2], ALU.mult, ALU.add),
                                 reads=[wb[5], tabb, cyb[u_]], writes=[wb[7]])
                            P.op(P.dve, lambda: nc.vector.tensor_tensor(t[0], t[6], TcO, ALU.mult), reads=[wb[6], tabb], writes=[wb[0]])
                            P.op(P.dve, lambda: nc.vector.tensor_tensor(t[1], t[7], TsO, ALU.mult), reads=[wb[7], tabb], writes=[wb[1]])
                            P.op(P.pool, lambda: nc.gpsimd.tensor_tensor(rv(t[8]), t[0], t[1], ALU.subtract), reads=[wb[0], wb[1]], writes=[wb[8]])
                            P.op(P.pool, lambda: nc.gpsimd.tensor_tensor(t[2], t[6], TsO, ALU.mult), reads=[wb[6], tabb], writes=[wb[2]])
                            P.op(P.pool, lambda: nc.gpsimd.tensor_tensor(t[3], t[7], TcO, ALU.mult), reads=[wb[7], tabb], writes=[wb[3]])
                            P.op(P.pool, lambda: nc.gpsimd.tensor_tensor(rv(t[9]), t[2], t[3], ALU.add), reads=[wb[2], wb[3]], writes=[wb[9]])
                            P.op(P.pool, lambda: nc.gpsimd.tensor_copy(carry[:, u_, 0:1], t[8][:, last:last + 1]), reads=[wb[8]], writes=[cyb[u_]])
                            P.op(P.pool, lambda: nc.gpsimd.tensor_copy(carry[:, u_, 1:2], t[9][:, last:last + 1]), reads=[wb[9]], writes=[cyb[u_]])
                            P.pe_deps([lbb, wb[8], wb[9]], [yb_] if gp == 0 else [])
                            nc.tensor.matmul(ybank, LB[:, 2, u_, :], t[8], start=(gp == 0), stop=False)
                            ins = nc.tensor.matmul(ybank, LB[:, 3, u_, :], t[9], start=False, stop=(gp == 3))
                            P.pe_done(ins, [wb[8], wb[9]], [yb_] if gp == 3 else [])
                        P.op(P.dve, lambda: nc.vector.tensor_tensor(ys_, ybank, ys_, ALU.add),
                             reads=[yb_, ysb], writes=[ysb])
                        P.dma(P.sp, yo[s][:, tsl], ys_, reads=[ysb], writes=[yob[s][ti]])
            P.barrier()
        return nc


def run_even_mixer(inputs, l, LP, LS, qkv, u, trace=False):
    import math
    import ml_dtypes
    bf = ml_dtypes.bfloat16
    g = lambda n: np.asarray(inputs[n])
    e = l // 2
    TP, TS = LP // NCORE, LS // NCORE
    Ls = [LP, LS]
    lambda_init = 0.8 - 0.6 * math.exp(-0.3 * l)
    iota = np.broadcast_to(np.arange(1, T + 1, dtype=np.float32)[None, :], (128, T)).copy()
    ident = np.eye(128, dtype=np.float32)
    in_maps = []
    for c in range(NCORE):
        spar = np.zeros((128, 32), np.float32)
        bc = np.zeros((128, 4, 8, 16), np.float32)
        for d in range(2):
            for gp in range(4):
                u_ = d * 4 + gp
                for gl in range(2):
                    gi = 8 * c + 2 * gp + gl
                    rows = slice(64 * gl, 64 * gl + 64)
                    spar[rows, u_] = g("s5_lambda_re")[e, d, gi]
                    spar[rows, 8 + u_] = g("s5_lambda_im")[e, d, gi]
                    spar[rows, 16 + u_] = g("s5_log_dt")[e, d, gi]
                    bc[rows, 0, u_, :] = g("s5_b_re")[e, d, gi]
                    bc[rows, 1, u_, :] = g("s5_b_im")[e, d, gi]
                    bc[rows, 2, u_, :] = g("s5_c_re")[e, d, gi].T
                    bc[rows, 3, u_, :] = g("s5_c_im")[e, d, gi].T
        spar[:, 24] = np.pi / 2
        spar[:, 25] = g("s5_d")[e, 128 * c:128 * (c + 1)]
        m = {"spar": spar, "bc": bc, "iota": iota, "ident": ident}
        for s in range(2):
            m[f"u{s}"] = np.ascontiguousarray(u[s][c])
        in_maps.append(m)
    res_s = _launch(("MS", LP, LS), lambda: KernMS(LP, LS).build(), in_maps, trace)
    kl = np.arange(128)[:, None]
    dist = np.stack([np.abs(np.arange(T)[None, :] - (128 * b + kl)) for b in range(4)], 1).astype(np.float32)
    in_maps = []
    for c in range(NCORE):
        slope = 2.0 ** (-8.0 * (c + 1) / 8)
        sig = slope * 8.0
        par = np.zeros((128, 4 * 64 + 8), np.float32)
        par[:, 0:64] = g("diff_lambda_q1")[e][None, :]
        par[:, 64:128] = g("diff_lambda_k1")[e][None, :]
        par[:, 128:192] = g("diff_lambda_q2")[e][None, :]
        par[:, 192:256] = g("diff_lambda_k2")[e][None, :]
        par[:, 256] = -sig
        par[:, 257] = g("diff_subln")[e]
        par[:, 258] = 1e-5
        m = {"dist": dist, "par": par}
        for s in range(2):
            L = Ls[s]
            pos = np.arange(L)
            hi, lo = (pos // 128).astype(np.float32), (pos % 128).astype(np.float32)
            kaug = np.stack([hi, lo, np.ones(L, np.float32), np.ones(L, np.float32)], 0)
            qbef = np.stack([np.full(L, 128 * sig, np.float32), np.full(L, sig, np.float32), -128 * sig * hi, -sig * lo], 0)
            qaug = [qbef, -qbef, np.zeros_like(qbef)]
            q, k, v = qkv[s][c], qkv[s][8 + c], qkv[s][16 + c]
            ka = np.empty((2, 68, L), bf)
            qa = np.empty((2, 3, 68, L), bf)
            for mm in range(2):
                ka[mm, :64] = k[64 * mm:64 * mm + 64]
                ka[mm, 64:] = kaug.astype(bf)
                for var in range(3):
                    qa[mm, var, :64] = q[64 * mm:64 * mm + 64]
                    qa[mm, var, 64:] = qaug[var].astype(bf)
            m[f"ka{s}"] = ka
            m[f"qa{s}"] = qa
            m[f"v{s}"] = np.ascontiguousarray(v.reshape(128, L // 128, 128).transpose(2, 1, 0))
        in_maps.append(m)
    res_d = _launch(("MD", LP, LS, l), lambda: KernMD(LP, LS, lambda_init).build(), in_maps, trace)
    mix = []
    for c in range(NCORE):
        yb = np.stack([np.concatenate([res_d[h]["o0"][:, c * TP:(c + 1) * TP], res_d[h]["o1"][:, c * TS:(c + 1) * TS]], 1)
                       for h in range(NCORE)], 0)
        ys = np.stack([np.concatenate([res_s[h]["y0"][:, c * TP:(c + 1) * TP], res_s[h]["y1"][:, c * TS:(c + 1) * TS]], 1)
                       for h in range(NCORE)], 0)
        mix.append({"yb_in": np.ascontiguousarray(yb), "ys5_in": np.ascontiguousarray(ys.astype(np.float32))})
    return mix


def make_consts():
    c = np.zeros((128, 384), np.float32)
    c[:, 0:128] = np.eye(128, dtype=np.float32)
    c[:, 128:256] = 1.0
    c[:, 256] = EPS
    c[:, 257] = np.pi / 2
    return c


def rope_consts():
    p = np.arange(128)
    j = p % 32
    invf = (10000.0 ** (-(2.0 * j) / 64.0)).astype(np.float32)
    sign = np.where((p % 64) < 32, -1.0, 1.0).astype(np.float32)
    perm = np.zeros((128, 128), np.float32)
    partner = np.where((p % 64) < 32, p + 32, p - 32)
    perm[partner, p] = 1.0
    return invf, sign, perm


def pos_table(t0, n):
    t = np.arange(t0, t0 + n)
    tab = np.empty((128, n), np.float32)
    tab[:64] = (t // 64)[None, :]
    tab[64:] = (t % 64)[None, :]
    return tab


_PROGS = {}


def _launch(key, builder, in_maps, trace=False):
    import time as _t
    _t0 = _t.time()
    if key not in _PROGS:
        _PROGS[key] = builder()
    nc = _PROGS[key]
    res = run_bass_kernel_spmd(nc, in_maps, core_ids=list(range(NCORE)), trace=trace)
    print("launch", key, "s", round(_t.time() - _t0, 1), "exec_ns", res.exec_time_ns, flush=True)
    return res.results


def run_model(inputs, LP, LS, layers, trace=False):
    import ml_dtypes
    bf = ml_dtypes.bfloat16
    TP, TS = LP // NCORE, LS // NCORE
    NTOK = TP + TS
    g = lambda n: np.asarray(inputs[n])
    xp, xs = g("x_prompt").reshape(LP, D), g("x_sample").reshape(LS, D)
    mem = np.ascontiguousarray(np.concatenate([g("mem_prompt").reshape(NMEM, D), g("mem_sample").reshape(NMEM, D)], 0))
    cst = make_consts()
    invf, sign, perm = rope_consts()
    nl = len(layers)
    xT = None
    mix = None
    for i in range(nl + 1):
        ops = []
        if i > 0:
            lp, kp = layers[i - 1]
            ops += ["mixout_even" if kp == "even" else "mixout_odd", "cross", "ffn2"]
        if i < nl:
            l, k = layers[i]
            ops += ["ffn1", "inproj_even" if k == "even" else "inproj_odd"]
        else:
            ops += ["final"]
        plan = {"first": i == 0, "ops": ops}
        small = np.zeros((128, 64), np.float32)
        small[:, 8], small[:, 9] = invf, sign
        shared = {"cst": cst, "perm": perm}
        if i > 0:
            shared["mem"] = mem
            for n in ("cross_w_q", "cross_w_kv", "cross_w_o", "cross_norm", "mem_norm", "ffn2_w_gu", "ffn2_w_down", "ffn2_norm"):
                shared[n] = np.ascontiguousarray(g(n)[lp])
            if kp == "even":
                e = lp // 2
                shared["mix_w_out"] = np.ascontiguousarray(g("even_w_out")[e])
                shared["s5_glu_w"] = np.ascontiguousarray(g("s5_glu_w")[e])
                small[:, 0:8] = g("s5_glu_b")[e].reshape(8, 128).T
            else:
                shared["mix_w_out"] = np.ascontiguousarray(g("odd_w_out")[lp // 2])
        if i < nl:
            for n in ("ffn1_w_gu", "ffn1_w_down", "ffn1_norm", "mix_norm"):
                shared[n] = np.ascontiguousarray(g(n)[l])
            if k == "even":
                shared["mix_w_in"] = np.ascontiguousarray(g("even_w_in")[l // 2])
            else:
                shared["mix_w_in"] = np.ascontiguousarray(g("odd_w_in")[l // 2])
                small[:, 11] = g("gqa_q_norm")[l // 2]
                small[:, 12] = g("gqa_k_norm")[l // 2]
        else:
            shared["final_norm"] = g("final_norm")
        shared["small"] = small
        in_maps = []
        for c in range(NCORE):
            m = dict(shared)
            if i == 0:
                m["x_tok"] = np.ascontiguousarray(np.concatenate([xp[c * TP:(c + 1) * TP], xs[c * TS:(c + 1) * TS]], 0))
            else:
                m["xT_in"] = xT[c]
                m.update(mix[c])
            if i < nl and k == "odd":
                m["pos_in"] = np.concatenate([pos_table(c * TP, TP), pos_table(c * TS, TS)], 1)
            in_maps.append(m)
        pl = plan

        def mk(pl=pl):
            kk = Kern(LP, LS)
            return kk.build(pl)
        res = _launch(("L", LP, LS, tuple(ops), i == 0), mk, in_maps, trace)
        if i == nl:
            yp = np.concatenate([res[c]["y_tok"][:TP] for c in range(NCORE)], 0).reshape(1, LP, D)
            ys = np.concatenate([res[c]["y_tok"][TP:] for c in range(NCORE)], 0).reshape(1, LS, D)
            return yp.astype(np.float32), ys.astype(np.float32)
        xT = [res[c]["xT_out"] for c in range(NCORE)]
        qkv = [np.concatenate([res[c]["qkv_out"][:, :, :TP] for c in range(NCORE)], 2),
               np.concatenate([res[c]["qkv_out"][:, :, TP:] for c in range(NCORE)], 2)]
        Ls = [LP, LS]
        if k == "odd":
            in_maps = []
            for c in range(NCORE):
                kh = c // 2
                h0 = 4 * kh + 2 * (c % 2)
                m = {}
                for s in range(2):
                    m[f"q{s}"] = np.ascontiguousarray(qkv[s][h0:h0 + 2])
                    m[f"k{s}"] = np.ascontiguousarray(qkv[s][16 + kh])
                    v = qkv[s][20 + kh]
                    m[f"v{s}"] = np.ascontiguousarray(v.reshape(128, Ls[s] // 128, 128).transpose(2, 1, 0))
                in_maps.append(m)
            res = _launch(("MO", LP, LS), lambda: KernMO(LP, LS).build(), in_maps, trace)
            mix = []
            o = [np.concatenate([res[c][f"o{s}"] for c in range(NCORE)], 0) for s in range(2)]
            for c in range(NCORE):
                yb = np.concatenate([o[0][:, :, c * TP:(c + 1) * TP], o[1][:, :, c * TS:(c + 1) * TS]], 2)
                mix.append({"yb_in": np.ascontiguousarray(yb)})
        else:
            u = [np.concatenate([res[c]["u_out"][:, :, :TP] for c in range(NCORE)], 2),
                 np.concatenate([res[c]["u_out"][:, :, TP:] for c in range(NCORE)], 2)]
            mix = run_even_mixer(inputs, l, LP, LS, qkv, u, trace)
    return None


def kernel(**inputs):
    layers = [(l, "even" if l % 2 == 0 else "odd") for l in range(4)]
    return run_model(inputs, 8192, 16384, layers)
```

```python
import numpy as np
import concourse.bass as bass
import concourse.mybir as mybir
from concourse.bass_utils import run_bass_kernel_spmd

F32 = mybir.dt.float32
BF16 = mybir.dt.bfloat16
AF = mybir.ActivationFunctionType
ALU = mybir.AluOpType

D = 2048
DC = 16
DFF = 5632
FC = 44
NMEM = 256
EPS = 1e-6
NCORE = 8
T = 512
SEM_LIM = 60000


class Ev:
    __slots__ = ("sem", "val")

    def __init__(self, sem, val):
        self.sem = sem
        self.val = val


class Buf:
    __slots__ = ("name", "w", "r")

    def __init__(self, name):
        self.name = name
        self.w = None
        self.r = []


class Eng:
    def __init__(self, P, name, h):
        self.P = P
        self.name = name
        self.h = h
        self.seen = {}
        self.sem = None
        self.cnt = 0
        self.nsem = 0
        self.pool = []
        self.pk = 0

    def new_sem(self):
        self.sem = self.P.nc.alloc_semaphore(f"s_{self.name}_{self.nsem}")
        self.nsem += 1
        self.cnt = 0

    def wait(self, ev):
        if ev is None:
            return
        k = ev.sem
        if self.seen.get(k, 0) >= ev.val:
            return
        self.h.wait_ge(ev.sem, ev.val)
        self.seen[k] = ev.val

    def mark(self, ins):
        if self.sem is None or self.cnt >= SEM_LIM:
            self.new_sem()
        self.cnt += 1
        ins.then_inc(self.sem, 1)
        return Ev(self.sem, self.cnt)

    def dma_mark_prepare(self):
        NP = 8
        if len(self.pool) < NP:
            self.pool.append([self.P.nc.alloc_semaphore(f"d_{self.name}_{len(self.pool)}_{self.nsem}"), 0])
            self.nsem += 1
            k = len(self.pool) - 1
        else:
            k = self.pk
            self.pk = (self.pk + 1) % NP
        ent = self.pool[k]
        if ent[1] > 0:
            self.wait(Ev(ent[0], ent[1]))
        if ent[1] + 16 > SEM_LIM:
            ent[0] = self.P.nc.alloc_semaphore(f"d_{self.name}_{k}_{self.nsem}")
            self.nsem += 1
            ent[1] = 0
        return ent

    def dma_mark(self, ent, ins):
        ent[1] += 16
        ins.then_inc(ent[0], 16)
        return Ev(ent[0], ent[1])


class Prog:
    def __init__(self, nc):
        self.nc = nc
        self.pe = Eng(self, "pe", nc.tensor)
        self.act = Eng(self, "act", nc.scalar)
        self.dve = Eng(self, "dve", nc.vector)
        self.sp = Eng(self, "sp", nc.sync)
        self.pool = Eng(self, "pool", nc.gpsimd)
        self.engs = [self.pe, self.act, self.dve, self.sp, self.pool]

    def _deps(self, eng, reads, writes):
        best = {}
        for b in reads:
            if b.w is not None:
                best[b.w.sem] = max(best.get(b.w.sem, 0), b.w.val)
        for b in writes:
            if b.w is not None:
                best[b.w.sem] = max(best.get(b.w.sem, 0), b.w.val)
            for e in b.r:
                best[e.sem] = max(best.get(e.sem, 0), e.val)
        for s, v in best.items():
            if eng.name == "pe" and eng.sem is not None and s == eng.sem:
                continue
            eng.wait(Ev(s, v))

    def _record(self, ev, reads, writes):
        for b in reads:
            nr = [e for e in b.r if e.sem is not ev.sem and e.sem != ev.sem]
            nr.append(ev)
            b.r = nr
        for b in writes:
            b.w = ev
            b.r = []

    def op(self, eng, fn, reads=(), writes=(), mark=True):
        self._deps(eng, reads, writes)
        ins = fn()
        if mark:
            ev = eng.mark(ins)
            self._record(ev, reads, writes)
            return ev
        return None

    def pe_deps(self, reads, writes):
        self._deps(self.pe, reads, writes)

    def pe_done(self, ins, reads, writes):
        ev = self.pe.mark(ins)
        self._record(ev, reads, writes)
        return ev

    def dma(self, eng, out, in_, reads=(), writes=(), slow=False):
        ent = eng.dma_mark_prepare()
        self._deps(eng, reads, writes)
        if slow:
            ins = eng.h.dma_start(out=out, in_=in_, allow_slow_non_contiguous=True)
        else:
            ins = eng.h.dma_start(out=out, in_=in_)
        ev = eng.dma_mark(ent, ins)
        self._record(ev, reads, writes)
        return ev

    def barrier(self):
        evs = []
        for e in self.engs:
            if e.sem is not None and e.cnt > 0:
                evs.append(Ev(e.sem, e.cnt))
            for ent in e.pool:
                if ent[1] > 0:
                    evs.append(Ev(ent[0], ent[1]))
        for e in self.engs:
            for ev in evs:
                e.wait(ev)


class WRing:
    def __init__(self, P, tile, nslots, slot_elems):
        self.P = P
        self.tile = tile
        self.n = nslots
        self.se = slot_elems
        self.bufs = [Buf(f"w{i}") for i in range(nslots)]
        self.queue = []
        self.loaded = []
        self.next_slot = 0
        self.issued = 0
        self.consumed = 0

    def enqueue(self, parts):
        self.queue.append(parts)

    def _issue_one(self):
        parts = self.queue[self.issued]
        s = self.issued % self.n
        b = self.bufs[s]
        off = 0
        views = []
        for (kc, ncols, src, rbufs) in parts:
            v = self.tile[:, s, off:off + kc * ncols].rearrange("p (k c) -> p k c", c=ncols)
            self.P.dma(self.P.sp, v, src, reads=rbufs, writes=[b])
            views.append(v)
            off += kc * ncols
        assert off <= self.se, (off, self.se)
        self.loaded.append((s, views))
        self.issued += 1

    def get(self):
        while self.issued < len(self.queue) and self.issued < self.consumed + self.n - 1:
            self._issue_one()
        s, views = self.loaded[self.consumed]
        self.consumed += 1
        return self.bufs[s], views


class PSum:
    def __init__(self, P, tile, banks=None):
        self.P = P
        self.tile = tile
        self.banks = list(range(8)) if banks is None else list(banks)
        self.bufs = {i: Buf(f"ps{i}") for i in range(8)}
        self.k = 0

    def get(self):
        k = self.banks[self.k]
        self.k = (self.k + 1) % len(self.banks)
        return self.bufs[k], self.tile[:, k * 512:(k + 1) * 512]

    def fixed(self, k):
        return self.bufs[k], self.tile[:, k * 512:(k + 1) * 512]


class Kern:
    def __init__(self, LP, LS, depth=4, stop=None):
        self.LP, self.LS, self.depth, self.stop = LP, LS, depth, stop
        self.TP, self.TS = LP // NCORE, LS // NCORE
        self.NTOK = self.TP + self.TS
        self.ntile = self.NTOK // T
        assert self.TP % T == 0 and self.TS % T == 0
        nc = bass.Bass("TRN2", target_bir_lowering=False)
        self.nc = nc
        self.P = Prog(nc)
        self.inp = {}

    def din(self, name, shape, dt=F32):
        t = self.nc.dram_tensor(name, list(shape), dt, kind="ExternalInput").ap()
        self.inp[name] = t
        return t

    def load_small(self, dst, src, buf):
        return self.P.dma(self.P.sp, dst, src, writes=[buf])

    def rmsnorm(self, gcol, gbuf):
        P, nc = self.P, self.nc
        bb, bank = self.ps.get()
        for c in range(DC):
            sqb = self.sqb[c % 2]
            sq = self.sq[:, c % 2, :]
            P.op(P.act, lambda: nc.scalar.activation(sq, self.x[:, c, :], AF.Square),
                 reads=[self.xb[c]], writes=[sqb])
            P.pe_deps([sqb, self.cb], [bb] if c == 0 else [])
            ins = nc.tensor.matmul(bank, self.ones[:], sq, start=(c == 0), stop=(c == DC - 1))
            P.pe_done(ins, [sqb], [bb] if c == DC - 1 else [])
        rs = self.rstd[:, :]
        P.op(P.act, lambda: nc.scalar.activation(rs, bank, AF.Sqrt, bias=self.epsc[:, 0:1], scale=1.0 / D),
             reads=[bb, self.cb], writes=[self.rsb])
        P.op(P.dve, lambda: nc.vector.reciprocal(rs, rs), reads=[self.rsb], writes=[self.rsb])
        for c in range(DC):
            P.op(P.dve, lambda: nc.vector.scalar_tensor_tensor(
                self.xn[:, c, :], self.x[:, c, :], gcol[:, c:c + 1], rs, ALU.mult, ALU.mult),
                reads=[self.xb[c], self.rsb, gbuf], writes=[self.xnb[c]])

    def proj(self, src, srcb, kc_n, w_ap, ncols_total, gran, epilogue, col0=0):
        P, nc = self.P, self.nc
        ng = ncols_total // gran
        for g in range(ng):
            wb, (wv,) = self.wr.get()
            for oi in range(gran // 128):
                bb, bank = self.ps.get()
                P.pe_deps([wb] + [srcb[k] for k in range(kc_n)], [bb])
                for k in range(kc_n):
                    ins = nc.tensor.matmul(bank, wv[:, k, oi * 128:(oi + 1) * 128], src[:, k, :],
                                           start=(k == 0), stop=(k == kc_n - 1))
                P.pe_done(ins, [wb] + [srcb[k] for k in range(kc_n)], [bb])
                epilogue(g * (gran // 128) + oi, bb, bank)

    def enq_proj(self, w_ap, kc_n, col0, ncols_total, gran):
        for g in range(ncols_total // gran):
            wbf, rb = self.wbf[w_ap.name]
            src = wbf[:, col0 + g * gran: col0 + (g + 1) * gran].rearrange("(k p) c -> p k c", p=128)
            self.wr.enqueue([(kc_n, gran, src, rb)])

    def enq_ffn(self, wgu, wdn):
        for j in range(FC // 2):
            self.enq_proj(wgu, DC, j * 256, 256, 256)
            self.enq_proj(wgu, DC, DFF + j * 256, 256, 256)
        self.enq_proj(wdn, FC, 0, D, 128)

    def ffn(self, gcol, gbuf, wgu, wdn):
        P, nc = self.P, self.nc
        self.rmsnorm(gcol, gbuf)
        for j2 in range(FC // 2):
            wgb, (wg,) = self.wr.get()
            wub, (wu,) = self.wr.get()
            for oi in range(2):
                j = j2 * 2 + oi
                gb, gbank = self.ps.get()
                ub, ubank = self.ps.get()
                P.pe_deps([wgb] + self.xnb, [gb])
                for k in range(DC):
                    ins = nc.tensor.matmul(gbank, wg[:, k, oi * 128:(oi + 1) * 128], self.xn[:, k, :],
                                           start=(k == 0), stop=(k == DC - 1))
                P.pe_done(ins, [wgb] + self.xnb, [gb])
                P.pe_deps([wub] + self.xnb, [ub])
                for k in range(DC):
                    ins = nc.tensor.matmul(ubank, wu[:, k, oi * 128:(oi + 1) * 128], self.xn[:, k, :],
                                           start=(k == 0), stop=(k == DC - 1))
                P.pe_done(ins, [wub] + self.xnb, [ub])
                tb = self.tmpb[j % 2]
                tmp = self.tmp[:, j % 2, :]
                P.op(P.act, lambda: nc.scalar.activation(tmp, gbank, AF.Silu), reads=[gb], writes=[tb])
                P.op(P.dve, lambda: nc.vector.tensor_tensor(self.sc[:, j, :], tmp, ubank, ALU.mult),
                     reads=[tb, ub], writes=[self.scb[j]])
        for i in range(DC):
            wb, (wv,) = self.wr.get()
            bb, bank = self.ps.get()
            P.pe_deps([wb] + self.scb[:FC], [bb])
            for j in range(FC):
                ins = nc.tensor.matmul(bank, wv[:, j, :], self.sc[:, j, :], start=(j == 0), stop=(j == FC - 1))
            P.pe_done(ins, [wb] + self.scb[:FC], [bb])
            P.op(P.dve, lambda: nc.vector.scalar_tensor_tensor(
                self.x[:, i, :], bank, 0.5, self.x[:, i, :], ALU.mult, ALU.add),
                reads=[bb, self.xb[i]], writes=[self.xb[i]])

    def load_x_tok(self, ti, src=None):
        P, nc = self.P, self.nc
        for tb in range(T // 128):
            r0 = ti * T + tb * 128
            P.dma(P.sp, self.tok[:, :], (self.x_tok if src is None else src)[r0:r0 + 128, :], writes=[self.tokb])
            for c4 in range(DC // 4):
                bb, bank = self.ps.get()
                P.pe_deps([self.tokb, self.cb], [bb])
                for q in range(4):
                    c = c4 * 4 + q
                    ins = nc.tensor.transpose(bank[:, q * 128:(q + 1) * 128], self.tok[:, c * 128:(c + 1) * 128],
                                              self.ident[:])
                P.pe_done(ins, [self.tokb], [bb])
                P.op(P.dve, lambda: nc.vector.tensor_copy(
                    self.x[:, c4 * 4:(c4 + 1) * 4, tb * 128:(tb + 1) * 128],
                    bank.rearrange("p (q t) -> p q t", q=4)),
                    reads=[bb], writes=[self.xb[c4 * 4 + q] for q in range(4)])

    def store_y_tok(self, ti, gcol, gbuf):
        P, nc = self.P, self.nc
        bb, bank = self.ps.get()
        for c in range(DC):
            sqb = self.sqb[c % 2]
            sq = self.sq[:, c % 2, :]
            P.op(P.act, lambda: nc.scalar.activation(sq, self.x[:, c, :], AF.Square),
                 reads=[self.xb[c]], writes=[sqb])
            P.pe_deps([sqb, self.cb], [bb] if c == 0 else [])
            ins = nc.tensor.matmul(bank, self.ones[:], sq, start=(c == 0), stop=(c == DC - 1))
            P.pe_done(ins, [sqb], [bb] if c == DC - 1 else [])
        rs = self.rstd[:, :]
        P.op(P.act, lambda: nc.scalar.activation(rs, bank, AF.Sqrt, bias=self.epsc[:, 0:1], scale=1.0 / D),
             reads=[bb, self.cb], writes=[self.rsb])
        P.op(P.dve, lambda: nc.vector.reciprocal(rs, rs), reads=[self.rsb], writes=[self.rsb])
        for c in range(DC):
            P.op(P.dve, lambda: nc.vector.scalar_tensor_tensor(
                self.x[:, c, :], self.x[:, c, :], gcol[:, c:c + 1], rs, ALU.mult, ALU.mult),
                reads=[self.xb[c], self.rsb, gbuf], writes=[self.xb[c]])
        for tb in range(T // 128):
            r0 = ti * T + tb * 128
            for c4 in range(DC // 4):
                bb, bank = self.ps.get()
                P.pe_deps([self.xb[c4 * 4 + q] for q in range(4)] + [self.cb], [bb])
                for q in range(4):
                    c = c4 * 4 + q
                    ins = nc.tensor.transpose(bank[:, q * 128:(q + 1) * 128], self.x[:, c, tb * 128:(tb + 1) * 128],
                                              self.ident[:])
                P.pe_done(ins, [self.xb[c4 * 4 + q] for q in range(4)], [bb])
                P.op(P.dve, lambda: nc.vector.tensor_copy(self.tok[:, c4 * 512:(c4 + 1) * 512], bank),
                     reads=[bb], writes=[self.tokb])
            P.dma(P.sp, self.y_tok[r0:r0 + 128, :], self.tok[:, :], reads=[self.tokb])

    def enq_memkv(self, wkv):
        self.enq_proj(wkv, DC, 0, D, 256)
        self.enq_proj(wkv, DC, D, D, 256)

    def memkv(self, gcol, gbuf, wkv):
        P, nc = self.P, self.nc
        self.load_x_tok(0, src=self.memin)
        self.rmsnorm(gcol, gbuf)

        def epi_k(oc, bb, bank):
            P.op(P.act, lambda: nc.scalar.copy(self.memK[:, oc, :], bank), reads=[bb], writes=[self.memKb])
        self.proj(self.xn, self.xnb, DC, None, D, 256, epi_k)
        for g in range(D // 256):
            wb, (wv,) = self.wr.get()
            for nb in range(4):
                bb, bank = self.ps.get()
                P.pe_deps([wb] + self.xnb, [bb])
                for k in range(DC):
                    ins = nc.tensor.matmul(bank[:, 0:256], self.xn[:, k, nb * 128:(nb + 1) * 128], wv[:, k, :],
                                           start=(k == 0), stop=(k == DC - 1))
                P.pe_done(ins, [wb] + self.xnb, [bb])
                P.op(P.act, lambda: nc.scalar.copy(self.memV[:, nb, g * 256:(g + 1) * 256], bank[:, 0:256]),
                     reads=[bb], writes=[self.memVb])

    def enq_cross(self, wq, wo):
        self.enq_proj(wq, DC, 0, D, 256)
        self.enq_proj(wo, DC, 0, D, 256)

    def cross(self, ti, gcol, gbuf):
        P, nc = self.P, self.nc
        m0 = 0 if ti < self.TP // T else 2
        self.rmsnorm(gcol, gbuf)
        qT = self.sc
        OT0 = 16

        def epi_q(oc, bb, bank):
            P.op(P.act, lambda: nc.scalar.copy(qT[:, oc, :], bank), reads=[bb], writes=[self.scb[oc]])
        self.proj(self.xn, self.xnb, DC, None, D, 256, epi_q)
        scale = 512.0 ** -0.5
        for h in range(4):
            eb = [self.scb[32 + (h % 2) * 2 + nb] for nb in range(2)]
            E = [self.sc[:, 32 + (h % 2) * 2 + nb, :] for nb in range(2)]
            for nb in range(2):
                bb, bank = self.ps.get()
                rd = [self.memKb] + [self.scb[h * 4 + dc] for dc in range(4)]
                P.pe_deps(rd, [bb])
                for dc in range(4):
                    ins = nc.tensor.matmul(bank, self.memK[:, h * 4 + dc, (m0 + nb) * 128:(m0 + nb + 1) * 128],
                                           qT[:, h * 4 + dc, :], start=(dc == 0), stop=(dc == 3))
                P.pe_done(ins, rd, [bb])
                P.op(P.act, lambda: nc.scalar.activation(E[nb], bank, AF.Exp, scale=scale),
                     reads=[bb], writes=[eb[nb]])
            zb, zbank = self.ps.get()
            P.pe_deps(eb + [self.cb2], [zb])
            for nb in range(2):
                ins = nc.tensor.matmul(zbank, self.onesb[:], E[nb], start=(nb == 0), stop=(nb == 1))
            P.pe_done(ins, eb, [zb])
            tb = self.tmpb[h % 2]
            rz = self.tmp[:, h % 2, :]
            P.op(P.dve, lambda: nc.vector.reciprocal(rz, zbank), reads=[zb], writes=[tb])
            for dc in range(4):
                bb, bank = self.ps.get()
                P.pe_deps(eb + [self.memVb], [bb])
                for nb in range(2):
                    ins = nc.tensor.matmul(bank, self.memV[:, m0 + nb, h * 512 + dc * 128: h * 512 + (dc + 1) * 128],
                                           E[nb], start=(nb == 0), stop=(nb == 1))
                P.pe_done(ins, eb + [self.memVb], [bb])
                oc = OT0 + h * 4 + dc
                P.op(P.dve, lambda: nc.vector.tensor_tensor(self.sc[:, oc, :], bank, rz, ALU.mult),
                     reads=[bb, tb], writes=[self.scb[oc]])

        def epi_o(oc, bb, bank):
            P.op(P.dve, lambda: nc.vector.tensor_tensor(self.x[:, oc, :], bank, self.x[:, oc, :], ALU.add),
                 reads=[bb, self.xb[oc]], writes=[self.xb[oc]])
        self.proj(self.sc[:, OT0:OT0 + DC, :], self.scb[OT0:OT0 + DC], DC, None, D, 256, epi_o)

    def enq_mixout(self, even):
        if even:
            self.enq_proj(self.W["s5_glu_w"], 8, 0, 1024, 256)
        self.enq_proj(self.W["mix_w_out"], DC, 0, D, 256)

    def mixout(self, ti, even):
        P, nc = self.P, self.nc
        ymix = self.sc
        tsl = slice(ti * T, (ti + 1) * T)
        if even:
            for c in range(8):
                P.dma(P.sp, ymix[:, 8 + c, :], self.yb_in[c, :, tsl], writes=[self.scb[8 + c]])
            G0 = 16
            for c in range(8):
                tb = self.tmpb[c % 2]
                tmp = self.tmp[:, c % 2, :]
                P.dma(P.sp, tmp, self.ys5_in[c, :, tsl], writes=[tb])
                P.op(P.act, lambda: nc.scalar.activation(self.sc[:, G0 + c, :], tmp, AF.Gelu),
                     reads=[tb], writes=[self.scb[G0 + c]])

            def epi_g(oc, bb, bank):
                tb = self.tmpb[oc % 2]
                tmp = self.tmp[:, oc % 2, :]
                P.op(P.act, lambda: nc.scalar.activation(tmp, bank, AF.Sigmoid, bias=self.glub[:, oc:oc + 1]),
                     reads=[bb, self.smallb], writes=[tb])
                P.op(P.dve, lambda: nc.vector.tensor_tensor(ymix[:, oc, :], tmp, self.sc[:, G0 + oc, :], ALU.mult),
                     reads=[tb, self.scb[G0 + oc]], writes=[self.scb[oc]])
            self.proj(self.sc[:, G0:G0 + 8, :], self.scb[G0:G0 + 8], 8, None, 1024, 256, epi_g)
        else:
            for c in range(DC):
                P.dma(P.sp, ymix[:, c, :], self.yb_in[c, :, tsl], writes=[self.scb[c]])

        def epi_o(oc, bb, bank):
            P.op(P.dve, lambda: nc.vector.tensor_tensor(self.x[:, oc, :], bank, self.x[:, oc, :], ALU.add),
                 reads=[bb, self.xb[oc]], writes=[self.xb[oc]])
        self.proj(ymix[:, 0:DC, :], self.scb[0:DC], DC, None, D, 256, epi_o)

    def sincos(self, ang, angb, wk, wkb, sin_out, cos_out, outb):
        P, nc = self.P, self.nc
        MAGIC = 12582912.0
        PI = float(np.pi)
        P.op(P.dve, lambda: nc.vector.tensor_scalar(wk, ang, 1.0 / (2 * PI), MAGIC, ALU.mult, ALU.add),
             reads=[angb], writes=[wkb])
        P.op(P.dve, lambda: nc.vector.tensor_scalar(wk, wk, MAGIC, -2 * PI, ALU.subtract, ALU.mult),
             reads=[wkb], writes=[wkb])
        P.op(P.dve, lambda: nc.vector.tensor_tensor(ang, ang, wk, ALU.add), reads=[angb, wkb], writes=[angb])
        P.op(P.act, lambda: nc.scalar.activation(sin_out, ang, AF.Sin), reads=[angb], writes=[outb])
        P.op(P.act, lambda: nc.scalar.activation(wk, ang, AF.Abs), reads=[angb], writes=[wkb])
        P.op(P.act, lambda: nc.scalar.activation(cos_out, wk, AF.Sin, bias=self.halfpi[:, 0:1], scale=-1.0),
             reads=[wkb, self.cb], writes=[outb])

    def enq_inproj(self, even):
        self.enq_proj(self.W["mix_w_in"], DC, 0, 4096 if even else 3072, 256)

    def out_chunk(self, dst_ap, src_bank, bb, k):
        P, nc = self.P, self.nc
        sb_, st = self.stgb[k % 2], self.stg[:, k % 2, :]
        P.op(P.act, lambda: nc.scalar.copy(st, src_bank), reads=[bb], writes=[sb_])
        P.dma(P.sp, dst_ap, st, reads=[sb_])

    def inproj_even(self, ti):
        P, nc = self.P, self.nc
        tsl = slice(ti * T, (ti + 1) * T)
        self.rmsnorm(self.gain("mix_norm"), self.gnb)

        def epi(oc, bb, bank):
            if oc < 8:
                tb, tmp = self.tmpb[oc % 2], self.tmp[:, oc % 2, :]
                P.op(P.act, lambda: nc.scalar.copy(tmp, bank), reads=[bb], writes=[tb])
                P.dma(P.sp, self.u_out[oc, :, tsl], tmp, reads=[tb])
            else:
                self.out_chunk(self.qkv_out[oc - 8, :, tsl], bank, bb, oc)
        self.proj(self.xn, self.xnb, DC, None, 4096, 256, epi)

    def inproj_odd(self, ti):
        P, nc = self.P, self.nc
        tsl = slice(ti * T, (ti + 1) * T)
        self.rmsnorm(self.gain("mix_norm"), self.gnb)
        rb = self.ropeb
        ang = self.tmp[:, 0, :]
        P.dma(P.sp, ang, self.pos_in[:, tsl], writes=[self.tmpb[0]])
        P.op(P.dve, lambda: nc.vector.tensor_scalar(ang, ang, self.ropec[:, 0:1], None, ALU.mult),
             reads=[self.tmpb[0], self.smallb], writes=[self.tmpb[0]])
        self.sincos(ang, self.tmpb[0], self.tmp[:, 1, :], self.tmpb[1], self.rt[:, 1, :], self.rt[:, 0, :], rb)
        P.op(P.dve, lambda: nc.vector.tensor_scalar(self.rt[:, 1, :], self.rt[:, 1, :], self.ropec[:, 1:2], None, ALU.mult),
             reads=[rb, self.smallb], writes=[rb])

        def epi(oc, bb, bank):
            if oc >= 20:
                self.out_chunk(self.qkv_out[oc, :, tsl], bank, bb, oc)
                return
            gcol = self.qkg[:, 0:1] if oc < 16 else self.qkg[:, 1:2]
            zs, zsb = self.wk[:, 0, :], self.wkb[0]
            sq, sqb = self.wk[:, 1, :], self.wkb[1]
            P.op(P.act, lambda: nc.scalar.copy(zs, bank), reads=[bb], writes=[zsb])
            P.op(P.act, lambda: nc.scalar.activation(sq, bank, AF.Square), reads=[bb], writes=[sqb])
            b2, bank2 = self.ps.get()
            P.pe_deps([sqb, self.cb], [b2])
            ins = nc.tensor.matmul(bank2, self.ones[:], sq, start=True, stop=True)
            P.pe_done(ins, [sqb], [b2])
            rs, rsb = self.wk[:, 2, :], self.wkb[2]
            P.op(P.act, lambda: nc.scalar.activation(rs, bank2, AF.Sqrt, bias=self.epsc[:, 0:1], scale=1.0 / 128),
                 reads=[b2, self.cb], writes=[rsb])
            P.op(P.dve, lambda: nc.vector.reciprocal(rs, rs), reads=[rsb], writes=[rsb])
            P.op(P.dve, lambda: nc.vector.scalar_tensor_tensor(zs, zs, gcol, rs, ALU.mult, ALU.mult),
                 reads=[zsb, rsb, self.smallb], writes=[zsb])
            b3, bank3 = self.ps.get()
            P.pe_deps([zsb, self.smallb], [b3])
            ins = nc.tensor.matmul(bank3, self.perm[:], zs, start=True, stop=True)
            P.pe_done(ins, [zsb], [b3])
            P.op(P.dve, lambda: nc.vector.tensor_tensor(sq, bank3, self.rt[:, 1, :], ALU.mult),
                 reads=[b3, rb], writes=[sqb])
            P.op(P.dve, lambda: nc.vector.tensor_tensor(zs, zs, self.rt[:, 0, :], ALU.mult),
                 reads=[zsb, rb], writes=[zsb])
            sb_, st = self.stgb[oc % 2], self.stg[:, oc % 2, :]
            P.op(P.dve, lambda: nc.vector.tensor_tensor(st, zs, sq, ALU.add), reads=[zsb, sqb], writes=[sb_])
            P.dma(P.sp, self.qkv_out[oc, :, tsl], st, reads=[sb_])
        self.proj(self.xn, self.xnb, DC, None, 3072, 256, epi)

    def gain(self, nm):
        return self.gn[:, self.gidx[nm], :]

    def build(self, plan):
        nc, P = self.nc, self.P
        NTOK = self.NTOK
        ops = plan["ops"]
        first = plan["first"]
        W = {}
        self.W = W
        if first:
            self.x_tok = self.din("x_tok", [NTOK, D])
        else:
            self.xT_in = self.din("xT_in", [DC, 128, NTOK])
        self.cst = self.din("cst", [128, 384])
        gnames = []

        def need(nm, shape, dt=F32):
            W[nm] = self.din(nm, shape, dt)
        if "ffn1" in ops:
            need("ffn1_w_gu", [D, 2 * DFF]); need("ffn1_w_down", [DFF, D]); gnames.append("ffn1_norm")
        if "ffn2" in ops:
            need("ffn2_w_gu", [D, 2 * DFF]); need("ffn2_w_down", [DFF, D]); gnames.append("ffn2_norm")
        if "cross" in ops:
            self.memin = self.din("mem", [2 * NMEM, D])
            need("cross_w_q", [D, D]); need("cross_w_kv", [D, 2 * D]); need("cross_w_o", [D, D])
            gnames += ["cross_norm", "mem_norm"]
        if "mixout_even" in ops or "mixout_odd" in ops:
            need("mix_w_out", [D, D])
            self.yb_in = self.din("yb_in", [8 if "mixout_even" in ops else DC, 128, NTOK], BF16)
        if "mixout_even" in ops:
            need("s5_glu_w", [1024, 1024])
            self.ys5_in = self.din("ys5_in", [8, 128, NTOK])
        if "inproj_even" in ops:
            need("mix_w_in", [D, 4096]); gnames.append("mix_norm")
            self.u_out = nc.dram_tensor("u_out", [8, 128, NTOK], F32, kind="ExternalOutput").ap()
            self.qkv_out = nc.dram_tensor("qkv_out", [24, 128, NTOK], BF16, kind="ExternalOutput").ap()
        if "inproj_odd" in ops:
            need("mix_w_in", [D, 3072]); gnames.append("mix_norm")
            self.pos_in = self.din("pos_in", [128, NTOK])
            self.qkv_out = nc.dram_tensor("qkv_out", [24, 128, NTOK], BF16, kind="ExternalOutput").ap()
        if "final" in ops:
            gnames.append("final_norm")
            self.y_tok = nc.dram_tensor("y_tok", [NTOK, D], F32, kind="ExternalOutput").ap()
        else:
            self.xT_out = nc.dram_tensor("xT_out", [DC, 128, NTOK], F32, kind="ExternalOutput").ap()
        self.small_in = self.din("small", [128, 64])
        self.perm_in = self.din("perm", [128, 128])
        for g in gnames:
            need(g, [D])
        self.gidx = {g: i for i, g in enumerate(gnames)}

        from contextlib import ExitStack
        with ExitStack() as es:
            def sb(name, shape, dt):
                return es.enter_context(nc.sbuf_tensor(name, shape, dt))
            self.pst = es.enter_context(nc.psum_tensor("pst", [128, 4096], F32))
            self.ps = PSum(P, self.pst)
            self.cs = sb("cs", [128, 384], F32)
            self.cb = Buf("cst")
            self.ident = self.cs[:, 0:128]
            self.ones = self.cs[:, 128:256]
            self.epsc = self.cs[:, 256:257]
            self.halfpi = self.cs[:, 257:258]
            P.dma(P.sp, self.cs[:, :], self.cst[:, :], writes=[self.cb])
            self.small = sb("small_sb", [128, 64], F32)
            self.smallb = Buf("small")
            P.dma(P.sp, self.small[:, :], self.small_in[:, :], writes=[self.smallb])
            self.glub = self.small[:, 0:8]
            self.ropec = self.small[:, 8:11]
            self.qkg = self.small[:, 11:13]
            self.perm = sb("perm_sb", [128, 128], F32)
            P.dma(P.sp, self.perm[:, :], self.perm_in[:, :], writes=[self.smallb])
            self.gn = sb("gn", [128, max(1, len(gnames)), DC], F32)
            self.gnb = Buf("gn")
            for g in gnames:
                P.dma(P.sp, self.gn[:, self.gidx[g], :], W[g].rearrange("(c p) -> p c", p=128),
                      writes=[self.gnb], slow=True)

            self.x = sb("x", [128, DC, T], F32)
            self.xb = [Buf(f"x{c}") for c in range(DC)]
            self.xn = sb("xn", [128, DC, T], BF16)
            self.xnb = [Buf(f"xn{c}") for c in range(DC)]
            self.sc = sb("sc", [128, FC, T], BF16)
            self.scb = [Buf(f"sc{c}") for c in range(FC)]
            self.sq = sb("sq", [128, 2, T], F32)
            self.sqb = [Buf(f"sq{c}") for c in range(2)]
            self.tmp = sb("tmp", [128, 2, T], F32)
            self.tmpb = [Buf(f"tmp{c}") for c in range(2)]
            self.rstd = sb("rstd", [128, T], F32)
            self.rsb = Buf("rstd")
            self.tok = sb("tok", [128, D], F32)
            self.tokb = Buf("tok")
            self.stg = sb("stg", [128, 2, T], BF16)
            self.stgb = [Buf(f"stg{c}") for c in range(2)]
            if "cross" in ops:
                self.memK = sb("memK", [128, DC, 2 * NMEM], BF16)
                self.memKb = Buf("memK")
                self.memV = sb("memV", [128, 4, D], BF16)
                self.memVb = Buf("memV")
            if "inproj_odd" in ops:
                self.rt = sb("rt", [128, 2, T], F32)
                self.ropeb = Buf("rt")
                self.wk = sb("wk", [128, 3, T], F32)
                self.wkb = [Buf(f"wk{c}") for c in range(3)]
            self.onesb = sb("onesb", [128, 128], BF16)
            self.cb2 = Buf("onesb")
            P.op(P.dve, lambda: nc.vector.tensor_copy(self.onesb[:], self.ones), reads=[self.cb], writes=[self.cb2])
            NSLOT, SLOT = 4, 6144
            self.wrt = sb("wrt", [128, NSLOT, SLOT], BF16)
            self.wr = WRing(P, self.wrt, NSLOT, SLOT)

            self.wbf = {}
            order = ["cross_w_kv", "s5_glu_w", "mix_w_out", "cross_w_q", "cross_w_o", "ffn2_w_gu", "ffn2_w_down",
                     "ffn1_w_gu", "ffn1_w_down", "mix_w_in"]
            for nm in order:
                if nm not in W:
                    continue
                K_, N_ = W[nm].shape
                wb_ = nc.dram_tensor(nm + "_bf", [K_, N_], BF16).ap()
                bufs = []
                RB = 256
                for r0 in range(0, K_, RB):
                    bb_ = Buf(f"{nm}_bf{r0}")
                    P.dma(P.pool, wb_[r0:r0 + RB, :], W[nm][r0:r0 + RB, :], writes=[bb_])
                    bufs.append(bb_)
                self.wbf[W[nm].name] = (wb_, bufs)
            if "cross" in ops:
                self.enq_memkv(W["cross_w_kv"])
                self.memkv(self.gain("mem_norm"), self.gnb, W["cross_w_kv"])
            for ti in range(self.ntile):
                for o in ops:
                    if o == "mixout_even": self.enq_mixout(True)
                    elif o == "mixout_odd": self.enq_mixout(False)
                    elif o == "cross": self.enq_cross(W["cross_w_q"], W["cross_w_o"])
                    elif o == "ffn1": self.enq_ffn(W["ffn1_w_gu"], W["ffn1_w_down"])
                    elif o == "ffn2": self.enq_ffn(W["ffn2_w_gu"], W["ffn2_w_down"])
                    elif o == "inproj_even": self.enq_inproj(True)
                    elif o == "inproj_odd": self.enq_inproj(False)
                if first:
                    self.load_x_tok(ti)
                else:
                    for c in range(DC):
                        P.dma(P.sp, self.x[:, c, :], self.xT_in[c, :, ti * T:(ti + 1) * T], writes=[self.xb[c]])
                for o in ops:
                    if o == "mixout_even": self.mixout(ti, True)
                    elif o == "mixout_odd": self.mixout(ti, False)
                    elif o == "cross": self.cross(ti, self.gain("cross_norm"), self.gnb)
                    elif o == "ffn1": self.ffn(self.gain("ffn1_norm"), self.gnb, None, None)
                    elif o == "ffn2": self.ffn(self.gain("ffn2_norm"), self.gnb, None, None)
                    elif o == "inproj_even": self.inproj_even(ti)
                    elif o == "inproj_odd": self.inproj_odd(ti)
                    elif o == "final": self.store_y_tok(ti, self.gain("final_norm"), self.gnb)
                if "final" not in ops:
                    for c in range(DC):
                        P.dma(P.sp, self.xT_out[c, :, ti * T:(ti + 1) * T], self.x[:, c, :], reads=[self.xb[c]])
            P.barrier()
        return nc


class KernMO:
    def __init__(self, LP, LS):
        self.Ls = [LP, LS]
        self.nc = bass.Bass("TRN2", target_bir_lowering=False)
        self.P = Prog(self.nc)

    def build(self):
        nc, P = self.nc, self.P
        LM = max(self.Ls)
        qi, ki, vi, oo = [], [], [], []
        for s, L in enumerate(self.Ls):
            qi.append(nc.dram_tensor(f"q{s}", [2, 128, L], BF16, kind="ExternalInput").ap())
            ki.append(nc.dram_tensor(f"k{s}", [128, L], BF16, kind="ExternalInput").ap())
            vi.append(nc.dram_tensor(f"v{s}", [128, L // 128, 128], BF16, kind="ExternalInput").ap())
            oo.append(nc.dram_tensor(f"o{s}", [2, 128, L], BF16, kind="ExternalOutput").ap())
        from contextlib import ExitStack
        with ExitStack() as es:
            def sb(name, shape, dt):
                return es.enter_context(nc.sbuf_tensor(name, shape, dt))
            pst = es.enter_context(nc.psum_tensor("pst", [128, 4096], F32))
            ps = PSum(P, pst, banks=[4, 5, 6, 7])
            kT = sb("kT", [128, LM], BF16); kTb = Buf("kT")
            V = sb("V", [128, LM // 128, 128], BF16); Vb = Buf("V")
            qt_ = sb("qt", [128, 2, 2, T], BF16); qb = [Buf("q0"), Buf("q1")]
            E = sb("E", [128, 4, T], BF16); Eb = [Buf(f"E{i}") for i in range(4)]
            onesb = sb("onesb", [128, 128], BF16); ob = Buf("ones")
            rz = sb("rz", [128, 2, T], F32); rzb = [Buf("rz0"), Buf("rz1")]
            ost = sb("ost", [128, 2, T], BF16); ostb = [Buf("os0"), Buf("os1")]
            P.op(P.dve, lambda: nc.vector.memset(onesb[:], 1.0), writes=[ob])
            scale = 128.0 ** -0.5
            ek = 0
            for s, L in enumerate(self.Ls):
                P.dma(P.sp, kT[:, 0:L], ki[s][:, :], writes=[kTb])
                P.dma(P.sp, V[:, 0:L // 128, :], vi[s][:, :, :], writes=[Vb])
                nkb = L // 128
                for qt in range(L // T):
                    qs = qt % 2
                    P.dma(P.sp, qt_[:, qs, :, :], qi[s][:, :, qt * T:(qt + 1) * T].rearrange("h p t -> p h t"),
                          writes=[qb[qs]])
                    items = [(kb, h) for kb in range(nkb) for h in range(2)]
                    sc_banks = {}

                    def emit_S(idx):
                        kb, h = items[idx]
                        b, bank = ps.get()
                        P.pe_deps([kTb, qb[qs]], [b])
                        ins = nc.tensor.matmul(bank, kT[:, kb * 128:(kb + 1) * 128], qt_[:, qs, h, :], start=True, stop=True)
                        P.pe_done(ins, [kTb, qb[qs]], [b])
                        sc_banks[idx] = (b, bank)
                    SK = 2
                    for i in range(min(SK, len(items))):
                        emit_S(i)
                    for idx, (kb, h) in enumerate(items):
                        if idx + SK < len(items):
                            emit_S(idx + SK)
                        b, bank = sc_banks.pop(idx)
                        e = ek % 4
                        ek += 1
                        P.op(P.act, lambda: nc.scalar.activation(E[:, e, :], bank, AF.Exp, scale=scale),
                             reads=[b], writes=[Eb[e]])
                        ab, abank = ps.fixed(2 * h)
                        zb, zbank = ps.fixed(2 * h + 1)
                        first, last = kb == 0, kb == nkb - 1
                        P.pe_deps([Eb[e], Vb, ob], [ab, zb] if first else [])
                        nc.tensor.matmul(abank, V[:, kb, :], E[:, e, :], start=first, stop=last)
                        ins = nc.tensor.matmul(zbank, onesb[:], E[:, e, :], start=first, stop=last)
                        P.pe_done(ins, [Eb[e], Vb], [ab, zb] if last else [])
                    for h in range(2):
                        ab, abank = ps.fixed(2 * h)
                        zb, zbank = ps.fixed(2 * h + 1)
                        P.op(P.dve, lambda: nc.vector.reciprocal(rz[:, h, :], zbank), reads=[zb], writes=[rzb[h]])
                        P.op(P.dve, lambda: nc.vector.tensor_tensor(ost[:, h, :], abank, rz[:, h, :], ALU.mult),
                             reads=[ab, rzb[h]], writes=[ostb[h]])
                        P.dma(P.sp, oo[s][h, :, qt * T:(qt + 1) * T], ost[:, h, :], reads=[ostb[h]])
            P.barrier()
        return nc


class KernMD:
    def __init__(self, LP, LS, lambda_init):
        self.Ls = [LP, LS]
        self.li = float(lambda_init)
        self.nc = bass.Bass("TRN2", target_bir_lowering=False)
        self.P = Prog(self.nc)

    def build(self):
        nc, P = self.nc, self.P
        LM = max(self.Ls)
        KR = 68
        ka, qa, vi, oo = [], [], [], []
        for s, L in enumerate(self.Ls):
            ka.append(nc.dram_tensor(f"ka{s}", [2, KR, L], BF16, kind="ExternalInput").ap())
            qa.append(nc.dram_tensor(f"qa{s}", [2, 3, KR, L], BF16, kind="ExternalInput").ap())
            vi.append(nc.dram_tensor(f"v{s}", [128, L // 128, 128], BF16, kind="ExternalInput").ap())
            oo.append(nc.dram_tensor(f"o{s}", [128, L], BF16, kind="ExternalOutput").ap())
        dist_in = nc.dram_tensor("dist", [128, 4, T], F32, kind="ExternalInput").ap()
        par_in = nc.dram_tensor("par", [128, 4 * 64 + 8], F32, kind="ExternalInput").ap()
        from contextlib import ExitStack
        with ExitStack() as es:
            def sb(name, shape, dt):
                return es.enter_context(nc.sbuf_tensor(name, shape, dt))
            pst = es.enter_context(nc.psum_tensor("pst", [128, 4096], F32))
            ps = PSum(P, pst, banks=[4, 5, 6, 7])
            KA = sb("KA", [KR, 2, LM], BF16); KAb = Buf("KA")
            V = sb("V", [128, LM // 128, 128], BF16); Vb = Buf("V")
            QT = sb("QT", [KR, 2, 2, 3, T], BF16); qb = [Buf("q0"), Buf("q1")]
            E = sb("E", [128, 4, T], BF16); Eb = [Buf(f"E{i}") for i in range(4)]
            DIST = sb("dist_sb", [128, 4, T], F32); Db = Buf("dist")
            par = sb("par_sb", [128, 4 * 64 + 8], F32); pb = Buf("par")
            onesb = sb("onesb", [128, 128], BF16); ob = Buf("ones")
            onesf = sb("onesf", [128, 128], F32)
            bt = sb("bt", [128, 2, T], F32); btb = [Buf("bt0"), Buf("bt1")]
            wk = sb("wk", [128, 4, T], F32); wkb = [Buf(f"wk{i}") for i in range(4)]
            ost = sb("ost", [128, 2, T], BF16); ostb = [Buf("os0"), Buf("os1")]
            col = sb("col", [128, 16], F32); cb_ = Buf("col")
            P.op(P.dve, lambda: nc.vector.memset(onesb[:], 1.0), writes=[ob])
            P.op(P.dve, lambda: nc.vector.memset(onesf[:], 1.0), writes=[ob])
            P.dma(P.sp, DIST[:, :, :], dist_in[:, :, :], writes=[Db])
            P.dma(P.sp, par[:, :], par_in[:, :], writes=[pb])
            pr = wk[:, 0, 0:128]
            P.op(P.dve, lambda: nc.vector.tensor_tensor(pr[:, 0:64], par[:, 0:64], par[:, 64:128], ALU.mult),
                 reads=[pb], writes=[wkb[0]])
            P.op(P.dve, lambda: nc.vector.tensor_tensor(pr[:, 64:128], par[:, 128:192], par[:, 192:256], ALU.mult),
                 reads=[pb], writes=[wkb[0]])
            P.op(P.dve, lambda: nc.vector.reduce_sum(col[:, 0:1], pr[:, 0:64], mybir.AxisListType.X),
                 reads=[wkb[0]], writes=[cb_])
            P.op(P.dve, lambda: nc.vector.reduce_sum(col[:, 1:2], pr[:, 64:128], mybir.AxisListType.X),
                 reads=[wkb[0]], writes=[cb_])
            P.op(P.act, lambda: nc.scalar.activation(col[:, 2:4], col[:, 0:2], AF.Exp), reads=[cb_], writes=[cb_])
            P.op(P.dve, lambda: nc.vector.tensor_tensor(col[:, 4:5], col[:, 3:4], col[:, 2:3], ALU.subtract),
                 reads=[cb_], writes=[cb_])
            P.op(P.dve, lambda: nc.vector.tensor_scalar(col[:, 5:6], col[:, 4:5], -self.li, None, ALU.add),
                 reads=[cb_], writes=[cb_])
            P.op(P.dve, lambda: nc.vector.tensor_scalar(col[:, 6:7], par[:, 257:258], 1.0 - self.li, None, ALU.mult),
                 reads=[pb, cb_], writes=[cb_])
            negsig = par[:, 256:257]
            neglam = col[:, 5:6]
            gsub = col[:, 6:7]
            epsc = par[:, 258:259]
            scale = 64.0 ** -0.5
            ek = 0
            for s, L in enumerate(self.Ls):
                for m in range(2):
                    P.dma(P.sp, KA[:, m, 0:L], ka[s][m, :, :], writes=[KAb])
                P.dma(P.sp, V[:, 0:L // 128, :], vi[s][:, :, :], writes=[Vb])
                nkb = L // 128
                for qt in range(L // T):
                    qs = qt % 2
                    for m in range(2):
                        P.dma(P.sp, QT[:, qs, m, :, :],
                              qa[s][m, :, :, qt * T:(qt + 1) * T].rearrange("v r t -> r v t"), writes=[qb[qs]])
                    items = [(kb, m) for kb in range(nkb) for m in range(2)]
                    sc_banks = {}

                    def emit_S(idx):
                        kb, m = items[idx]
                        rel = kb - 4 * qt
                        var = 0 if rel < 0 else (1 if rel > 3 else 2)
                        b, bank = ps.get()
                        P.pe_deps([KAb, qb[qs]], [b])
                        ins = nc.tensor.matmul(bank, KA[:, m, kb * 128:(kb + 1) * 128], QT[:, qs, m, var, :],
                                               start=True, stop=True)
                        P.pe_done(ins, [KAb, qb[qs]], [b])
                        sc_banks[idx] = (b, bank)
                    SK = 2
                    for i in range(min(SK, len(items))):
                        emit_S(i)
                    for idx, (kb, m) in enumerate(items):
                        if idx + SK < len(items):
                            emit_S(idx + SK)
                        b, bank = sc_banks.pop(idx)
                        rel = kb - 4 * qt
                        e = ek % 4
                        ek += 1
                        if 0 <= rel <= 3:
                            tb, tt = btb[ek % 2], bt[:, ek % 2, :]
                            P.op(P.dve, lambda: nc.vector.scalar_tensor_tensor(tt, DIST[:, rel, :], negsig, bank,
                                                                               ALU.mult, ALU.add),
                                 reads=[b, Db, pb], writes=[tb])
                            P.op(P.act, lambda: nc.scalar.activation(E[:, e, :], tt, AF.Exp, scale=scale),
                                 reads=[tb], writes=[Eb[e]])
                        else:
                            P.op(P.act, lambda: nc.scalar.activation(E[:, e, :], bank, AF.Exp, scale=scale),
                                 reads=[b], writes=[Eb[e]])
                        ab, abank = ps.fixed(2 * m)
                        zb, zbank = ps.fixed(2 * m + 1)
                        first, last = kb == 0, kb == nkb - 1
                        P.pe_deps([Eb[e], Vb, ob], [ab, zb] if first else [])
                        nc.tensor.matmul(abank, V[:, kb, :], E[:, e, :], start=first, stop=last)
                        ins = nc.tensor.matmul(zbank, onesb[:], E[:, e, :], start=first, stop=last)
                        P.pe_done(ins, [Eb[e], Vb], [ab, zb] if last else [])
                    a0b, a0 = ps.fixed(0); z0b, z0 = ps.fixed(1); a1b, a1 = ps.fixed(2); z1b, z1 = ps.fixed(3)
                    P.op(P.dve, lambda: nc.vector.reciprocal(wk[:, 0, :], z0), reads=[z0b], writes=[wkb[0]])
                    P.op(P.dve, lambda: nc.vector.reciprocal(wk[:, 1, :], z1), reads=[z1b], writes=[wkb[1]])
                    P.op(P.dve, lambda: nc.vector.tensor_tensor(wk[:, 0, :], a0, wk[:, 0, :], ALU.mult),
                         reads=[a0b, wkb[0]], writes=[wkb[0]])
                    P.op(P.dve, lambda: nc.vector.scalar_tensor_tensor(wk[:, 1, :], a1, neglam, wk[:, 1, :],
                                                                       ALU.mult, ALU.mult),
                         reads=[a1b, wkb[1], cb_], writes=[wkb[1]])
                    P.op(P.dve, lambda: nc.vector.tensor_tensor(wk[:, 0, :], wk[:, 0, :], wk[:, 1, :], ALU.add),
                         reads=[wkb[0], wkb[1]], writes=[wkb[0]])
                    P.op(P.act, lambda: nc.scalar.activation(wk[:, 2, :], wk[:, 0, :], AF.Square),
                         reads=[wkb[0]], writes=[wkb[2]])
                    b2, bank2 = ps.get()
                    P.pe_deps([wkb[2], ob], [b2])
                    ins = nc.tensor.matmul(bank2, onesf[:], wk[:, 2, :], start=True, stop=True)
                    P.pe_done(ins, [wkb[2]], [b2])
                    P.op(P.act, lambda: nc.scalar.activation(wk[:, 3, :], bank2, AF.Sqrt, bias=epsc, scale=1.0 / 128),
                         reads=[b2, pb], writes=[wkb[3]])
                    P.op(P.dve, lambda: nc.vector.reciprocal(wk[:, 3, :], wk[:, 3, :]), reads=[wkb[3]], writes=[wkb[3]])
                    os_, osb = ost[:, qt % 2, :], ostb[qt % 2]
                    P.op(P.dve, lambda: nc.vector.scalar_tensor_tensor(os_, wk[:, 0, :], gsub, wk[:, 3, :],
                                                                       ALU.mult, ALU.mult),
                         reads=[wkb[0], wkb[3], cb_], writes=[osb])
                    P.dma(P.sp, oo[s][:, qt * T:(qt + 1) * T], os_, reads=[osb])
            P.barrier()
        return nc


class KernMS:
    def __init__(self, LP, LS):
        self.Ls = [LP, LS]
        self.nc = bass.Bass("TRN2", target_bir_lowering=False)
        self.P = Prog(self.nc)

    def build(self):
        nc, P = self.nc, self.P
        LM = max(self.Ls)
        ui, yo = [], []
        for s, L in enumerate(self.Ls):
            ui.append(nc.dram_tensor(f"u{s}", [128, L], F32, kind="ExternalInput").ap())
            yo.append(nc.dram_tensor(f"y{s}", [128, L], F32, kind="ExternalOutput").ap())
        spar_in = nc.dram_tensor("spar", [128, 32], F32, kind="ExternalInput").ap()
        bc_in = nc.dram_tensor("bc", [128, 4, 8, 16], F32, kind="ExternalInput").ap()
        iota_in = nc.dram_tensor("iota", [128, T], F32, kind="ExternalInput").ap()
        id_in = nc.dram_tensor("ident", [128, 128], F32, kind="ExternalInput").ap()
        from contextlib import ExitStack
        with ExitStack() as es:
            def sb(name, shape, dt):
                return es.enter_context(nc.sbuf_tensor(name, shape, dt))
            pst = es.enter_context(nc.psum_tensor("pst", [128, 4096], F32))
            ps = PSum(P, pst, banks=[2, 3, 4, 5, 6, 7])
            spar = sb("spar_sb", [128, 32], F32); spb = Buf("spar")
            bc = sb("bc_sb", [128, 4, 8, 16], F32); bcb = Buf("bc")
            iota = sb("iota_sb", [128, T], F32); iob = Buf("iota")
            ident = sb("ident_sb", [128, 128], F32); idb = Buf("ident")
            P.dma(P.sp, spar[:, :], spar_in[:, :], writes=[spb])
            P.dma(P.sp, bc[:, :, :, :], bc_in[:, :, :, :], writes=[bcb])
            P.dma(P.sp, iota[:, :], iota_in[:, :], writes=[iob])
            P.dma(P.sp, ident[:, :], id_in[:, :], writes=[idb])
            cw = sb("cw", [128, 24, 8], F32); cwb = Buf("cw")
            TAB = sb("TAB", [128, 5, 8, T], F32); tabb = Buf("tab")
            LB = sb("LB", [128, 4, 8, 128], F32); lbb = Buf("LB")
            BM = sb("BM", [128, 128], F32); bmb = Buf("BM")
            ysum = sb("ysum", [128, LM], F32); ysb = Buf("ysum")
            wkt = sb("wkt", [128, 10, T], F32); wb = [Buf(f"wk{i}") for i in range(10)]
            ut = sb("ut", [128, 2, T], F32); utb = [Buf("u0"), Buf("u1")]
            carry = sb("carry", [128, 8, 2], F32); cyb = [Buf(f"cy{i}") for i in range(8)]
            halfpi = spar[:, 24:25]
            MAGIC = 12582912.0
            PI = float(np.pi)

            def sincos(ang, angb, wk_, wkb_, sin_out, cos_out, outb):
                P.op(P.dve, lambda: nc.vector.tensor_scalar(wk_, ang, 1.0 / (2 * PI), MAGIC, ALU.mult, ALU.add),
                     reads=[angb], writes=[wkb_])
                P.op(P.dve, lambda: nc.vector.tensor_scalar(wk_, wk_, MAGIC, -2 * PI, ALU.subtract, ALU.mult),
                     reads=[wkb_], writes=[wkb_])
                P.op(P.dve, lambda: nc.vector.tensor_tensor(ang, ang, wk_, ALU.add), reads=[angb, wkb_], writes=[angb])
                P.op(P.act, lambda: nc.scalar.activation(sin_out, ang, AF.Sin), reads=[angb], writes=[outb])
                P.op(P.act, lambda: nc.scalar.activation(wk_, ang, AF.Abs), reads=[angb], writes=[wkb_])
                P.op(P.act, lambda: nc.scalar.activation(cos_out, wk_, AF.Sin, bias=halfpi, scale=-1.0),
                     reads=[wkb_, spb], writes=[outb])

            C = lambda i: cw[:, i, :]
            lamre, lamim, logdt = spar[:, 0:8], spar[:, 8:16], spar[:, 16:24]

            def cop(fn, reads=()):
                P.op(P.dve, fn, reads=[cwb, spb], writes=[cwb])
            LR, DT, LDT, RR, TH, SN, CS, ABRE, ABIM, NR, DEN, T1, T2, FRE, FIM, NFRE, W1, W2 = range(18)
            cop(lambda: nc.vector.tensor_scalar_min(C(LR), lamre, -1e-4))
            P.op(P.act, lambda: nc.scalar.activation(C(DT), logdt, AF.Exp), reads=[spb], writes=[cwb])
            cop(lambda: nc.vector.tensor_tensor(C(LDT), C(LR), C(DT), ALU.mult))
            P.op(P.act, lambda: nc.scalar.activation(C(RR), C(LDT), AF.Exp), reads=[cwb], writes=[cwb])
            cop(lambda: nc.vector.tensor_tensor(C(TH), lamim, C(DT), ALU.mult))
            cop(lambda: nc.vector.tensor_copy(C(W1), C(TH)))
            sincos(C(W1), cwb, C(W2), cwb, C(SN), C(CS), cwb)
            cop(lambda: nc.vector.tensor_tensor(C(ABRE), C(RR), C(CS), ALU.mult))
            cop(lambda: nc.vector.tensor_tensor(C(ABIM), C(RR), C(SN), ALU.mult))
            cop(lambda: nc.vector.tensor_scalar(C(NR), C(ABRE), -1.0, None, ALU.add))
            cop(lambda: nc.vector.tensor_tensor(C(DEN), C(LR), C(LR), ALU.mult))
            cop(lambda: nc.vector.tensor_tensor(C(T1), lamim, lamim, ALU.mult))
            cop(lambda: nc.vector.tensor_tensor(C(DEN), C(DEN), C(T1), ALU.add))
            cop(lambda: nc.vector.reciprocal(C(DEN), C(DEN)))
            cop(lambda: nc.vector.tensor_tensor(C(T1), C(NR), C(LR), ALU.mult))
            cop(lambda: nc.vector.tensor_tensor(C(T2), C(ABIM), lamim, ALU.mult))
            cop(lambda: nc.vector.tensor_tensor(C(T1), C(T1), C(T2), ALU.add))
            cop(lambda: nc.vector.tensor_tensor(C(FRE), C(T1), C(DEN), ALU.mult))
            cop(lambda: nc.vector.tensor_tensor(C(T1), C(ABIM), C(LR), ALU.mult))
            cop(lambda: nc.vector.tensor_tensor(C(T2), C(NR), lamim, ALU.mult))
            cop(lambda: nc.vector.tensor_tensor(C(T1), C(T1), C(T2), ALU.subtract))
            cop(lambda: nc.vector.tensor_tensor(C(FIM), C(T1), C(DEN), ALU.mult))
            cop(lambda: nc.vector.tensor_scalar(C(NFRE), C(FRE), -1.0, None, ALU.mult))
            for u_ in range(8):
                gp = u_ % 4
                ang, angb = wkt[:, 0, :], wb[0]
                w2, w2b = wkt[:, 1, :], wb[1]
                P.op(P.dve, lambda: nc.vector.tensor_scalar(ang, iota[:, :], cw[:, TH, u_:u_ + 1], None, ALU.mult),
                     reads=[iob, cwb], writes=[angb])
                sincos(ang, angb, w2, w2b, TAB[:, 3, u_, :], TAB[:, 2, u_, :], tabb)
                cc, ss = TAB[:, 2, u_, :], TAB[:, 3, u_, :]
                P.op(P.dve, lambda: nc.vector.tensor_scalar(w2, cc, cw[:, FRE, u_:u_ + 1], None, ALU.mult),
                     reads=[tabb, cwb], writes=[w2b])
                P.op(P.dve, lambda: nc.vector.scalar_tensor_tensor(TAB[:, 0, u_, :], ss, cw[:, FIM, u_:u_ + 1], w2,
                                                                   ALU.mult, ALU.add),
                     reads=[tabb, cwb, w2b], writes=[tabb])
                P.op(P.dve, lambda: nc.vector.tensor_scalar(w2, cc, cw[:, FIM, u_:u_ + 1], None, ALU.mult),
                     reads=[tabb, cwb], writes=[w2b])
                P.op(P.dve, lambda: nc.vector.scalar_tensor_tensor(TAB[:, 1, u_, :], ss, cw[:, NFRE, u_:u_ + 1], w2,
                                                                   ALU.mult, ALU.add),
                     reads=[tabb, cwb, w2b], writes=[tabb])
                P.op(P.dve, lambda: nc.vector.tensor_scalar(TAB[:, 4, u_, :], iota[:, :], 0.0, cw[:, RR, u_:u_ + 1],
                                                            ALU.mult, ALU.add),
                     reads=[iob, cwb], writes=[tabb])
                for k in range(4):
                    P.op(P.dve, lambda: nc.vector.memset(BM[:, :], 0.0), writes=[bmb])
                    for gl in range(2):
                        c0 = 16 * (2 * gp + gl)
                        if k == 3:
                            P.op(P.dve, lambda: nc.vector.tensor_scalar(BM[64 * gl:64 * gl + 64, c0:c0 + 16],
                                                                        bc[64 * gl:64 * gl + 64, k, u_, :], -1.0, None, ALU.mult),
                                 reads=[bcb], writes=[bmb])
                        else:
                            P.op(P.dve, lambda: nc.vector.tensor_copy(BM[64 * gl:64 * gl + 64, c0:c0 + 16],
                                                                      bc[64 * gl:64 * gl + 64, k, u_, :]),
                                 reads=[bcb], writes=[bmb])
                    if k < 2:
                        b, bank = ps.get()
                        P.pe_deps([bmb, idb], [b])
                        ins = nc.tensor.transpose(bank[:, 0:128], BM[:, :], ident[:, :])
                        P.pe_done(ins, [bmb], [b])
                        P.op(P.act, lambda: nc.scalar.copy(LB[:, k, u_, :], bank[:, 0:128]), reads=[b], writes=[lbb])
                    else:
                        P.op(P.act, lambda: nc.scalar.copy(LB[:, k, u_, :], BM[:, :]), reads=[bmb], writes=[lbb])
            dcol = spar[:, 25:26]
            uk = 0
            for s, L in enumerate(self.Ls):
                nt = L // T
                for u_ in range(8):
                    P.op(P.dve, lambda: nc.vector.memset(carry[:, u_, :], 0.0), writes=[cyb[u_]])
                for d in range(2):
                    order = list(range(nt)) if d == 0 else list(range(nt - 1, -1, -1))
                    rv = (lambda ap: ap) if d == 0 else (lambda ap: ap[:, ::-1])
                    last = T - 1 if d == 0 else 0
                    for ti in order:
                        tsl = slice(ti * T, (ti + 1) * T)
                        us, usb = ut[:, uk % 2, :], utb[uk % 2]
                        uk += 1
                        P.dma(P.sp, us, ui[s][:, tsl], writes=[usb])
                        if d == 0:
                            P.op(P.dve, lambda: nc.vector.tensor_scalar(ysum[:, tsl], us, dcol, None, ALU.mult),
                                 reads=[usb, spb], writes=[ysb])
                        yb_, ybank = ps.fixed(uk % 2)
                        for gp in range(4):
                            u_ = d * 4 + gp
                            b1, bank1 = ps.get()
                            b2, bank2 = ps.get()
                            P.pe_deps([lbb, usb], [b1])
                            ins = nc.tensor.matmul(bank1, LB[:, 0, u_, :], us, start=True, stop=True)
                            P.pe_done(ins, [lbb, usb], [b1])
                            P.pe_deps([lbb, usb], [b2])
                            ins = nc.tensor.matmul(bank2, LB[:, 1, u_, :], us, start=True, stop=True)
                            P.pe_done(ins, [lbb, usb], [b2])
                            TcI, TsI, TcO, TsO, Rt = (TAB[:, k, u_, :] for k in range(5))
                            t = [wkt[:, k, :] for k in range(10)]
                            P.op(P.dve, lambda: nc.vector.tensor_tensor(t[0], rv(bank1), TcI, ALU.mult), reads=[b1, tabb], writes=[wb[0]])
                            P.op(P.dve, lambda: nc.vector.tensor_tensor(t[1], rv(bank2), TsI, ALU.mult), reads=[b2, tabb], writes=[wb[1]])
                            P.op(P.dve, lambda: nc.vector.tensor_tensor(t[2], rv(bank1), TsI, ALU.mult), reads=[b1, tabb], writes=[wb[2]])
                            P.op(P.dve, lambda: nc.vector.tensor_tensor(t[3], rv(bank2), TcI, ALU.mult), reads=[b2, tabb], writes=[wb[3]])
                            P.op(P.pool, lambda: nc.gpsimd.tensor_tensor(t[4], t[0], t[1], ALU.subtract), reads=[wb[0], wb[1]], writes=[wb[4]])
                            P.op(P.pool, lambda: nc.gpsimd.tensor_tensor(t[5], t[2], t[3], ALU.add), reads=[wb[2], wb[3]], writes=[wb[5]])
                            P.op(P.dve, lambda: nc.vector.tensor_tensor_scan(t[6], Rt, t[4], carry[:, u_, 0:1], ALU.mult, ALU.add),
                                 reads=[wb[4], tabb, cyb[u_]], writes=[wb[6]])
                            P.op(P.dve, lambda: nc.vector.tensor_tensor_scan(t[7], Rt, t[5], carry[:, u_, 1:2], ALU.mult, ALU.add),
                                 reads=[wb[5], tabb, cyb[u_]], writes=[wb[7]])
                            P.op(P.pool, lambda: nc.gpsimd.tensor_tensor(t[0], t[6], TcO, ALU.mult), reads=[wb[6], tabb], writes=[wb[0]])
                            P.op(P.pool, lambda: nc.gpsimd.tensor_tensor(t[1], t[7], TsO, ALU.mult), reads=[wb[7], tabb], writes=[wb[1]])
                            P.op(P.pool, lambda: nc.gpsimd.tensor_tensor(rv(t[8]), t[0], t[1], ALU.subtract), reads=[wb[0], wb[1]], writes=[wb[8]])
                            P.op(P.pool, lambda: nc.gpsimd.tensor_tensor(t[2], t[6], TsO, ALU.mult), reads=[wb[6], tabb], writes=[wb[2]])
                            P.op(P.pool, lambda: nc.gpsimd.tensor_tensor(t[3], t[7], TcO, ALU.mult), reads=[wb[7], tabb], writes=[wb[3]])
                            P.op(P.pool, lambda: nc.gpsimd.tensor_tensor(rv(t[9]), t[2], t[3], ALU.add), reads=[wb[2], wb[3]], writes=[wb[9]])
                            P.op(P.pool, lambda: nc.gpsimd.tensor_copy(carry[:, u_, 0:1], t[8][:, last:last + 1]), reads=[wb[8]], writes=[cyb[u_]])
                            P.op(P.pool, lambda: nc.gpsimd.tensor_copy(carry[:, u_, 1:2], t[9][:, last:last + 1]), reads=[wb[9]], writes=[cyb[u_]])
                            P.pe_deps([lbb, wb[8], wb[9]], [yb_] if gp == 0 else [])
                            nc.tensor.matmul(ybank, LB[:, 2, u_, :], t[8], start=(gp == 0), stop=False)
                            ins = nc.tensor.matmul(ybank, LB[:, 3, u_, :], t[9], start=False, stop=(gp == 3))
                            P.pe_done(ins, [wb[8], wb[9]], [yb_] if gp == 3 else [])
                        P.op(P.dve, lambda: nc.vector.tensor_tensor(ysum[:, tsl], ybank, ysum[:, tsl], ALU.add),
                             reads=[yb_, ysb], writes=[ysb])
                P.dma(P.sp, yo[s][:, :], ysum[:, 0:L], reads=[ysb])
            P.barrier()
        return nc


def run_even_mixer(inputs, l, LP, LS, qkv, u, trace=False):
    import math
    import ml_dtypes
    bf = ml_dtypes.bfloat16
    g = lambda n: np.asarray(inputs[n])
    e = l // 2
    TP, TS = LP // NCORE, LS // NCORE
    Ls = [LP, LS]
    lambda_init = 0.8 - 0.6 * math.exp(-0.3 * l)
    iota = np.broadcast_to(np.arange(1, T + 1, dtype=np.float32)[None, :], (128, T)).copy()
    ident = np.eye(128, dtype=np.float32)
    in_maps = []
    for c in range(NCORE):
        spar = np.zeros((128, 32), np.float32)
        bc = np.zeros((128, 4, 8, 16), np.float32)
        for d in range(2):
            for gp in range(4):
                u_ = d * 4 + gp
                for gl in range(2):
                    gi = 8 * c + 2 * gp + gl
                    rows = slice(64 * gl, 64 * gl + 64)
                    spar[rows, u_] = g("s5_lambda_re")[e, d, gi]
                    spar[rows, 8 + u_] = g("s5_lambda_im")[e, d, gi]
                    spar[rows, 16 + u_] = g("s5_log_dt")[e, d, gi]
                    bc[rows, 0, u_, :] = g("s5_b_re")[e, d, gi]
                    bc[rows, 1, u_, :] = g("s5_b_im")[e, d, gi]
                    bc[rows, 2, u_, :] = g("s5_c_re")[e, d, gi].T
                    bc[rows, 3, u_, :] = g("s5_c_im")[e, d, gi].T
        spar[:, 24] = np.pi / 2
        spar[:, 25] = g("s5_d")[e, 128 * c:128 * (c + 1)]
        m = {"spar": spar, "bc": bc, "iota": iota, "ident": ident}
        for s in range(2):
            m[f"u{s}"] = np.ascontiguousarray(u[s][c])
        in_maps.append(m)
    res_s = _launch(("MS", LP, LS), lambda: KernMS(LP, LS).build(), in_maps, trace)
    kl = np.arange(128)[:, None]
    dist = np.stack([np.abs(np.arange(T)[None, :] - (128 * b + kl)) for b in range(4)], 1).astype(np.float32)
    in_maps = []
    for c in range(NCORE):
        slope = 2.0 ** (-8.0 * (c + 1) / 8)
        sig = slope * 8.0
        par = np.zeros((128, 4 * 64 + 8), np.float32)
        par[:, 0:64] = g("diff_lambda_q1")[e][None, :]
        par[:, 64:128] = g("diff_lambda_k1")[e][None, :]
        par[:, 128:192] = g("diff_lambda_q2")[e][None, :]
        par[:, 192:256] = g("diff_lambda_k2")[e][None, :]
        par[:, 256] = -sig
        par[:, 257] = g("diff_subln")[e]
        par[:, 258] = 1e-5
        m = {"dist": dist, "par": par}
        for s in range(2):
            L = Ls[s]
            pos = np.arange(L)
            hi, lo = (pos // 128).astype(np.float32), (pos % 128).astype(np.float32)
            kaug = np.stack([hi, lo, np.ones(L, np.float32), np.ones(L, np.float32)], 0)
            qbef = np.stack([np.full(L, 128 * sig, np.float32), np.full(L, sig, np.float32), -128 * sig * hi, -sig * lo], 0)
            qaug = [qbef, -qbef, np.zeros_like(qbef)]
            q, k, v = qkv[s][c], qkv[s][8 + c], qkv[s][16 + c]
            ka = np.empty((2, 68, L), bf)
            qa = np.empty((2, 3, 68, L), bf)
            for mm in range(2):
                ka[mm, :64] = k[64 * mm:64 * mm + 64]
                ka[mm, 64:] = kaug.astype(bf)
                for var in range(3):
                    qa[mm, var, :64] = q[64 * mm:64 * mm + 64]
                    qa[mm, var, 64:] = qaug[var].astype(bf)
            m[f"ka{s}"] = ka
            m[f"qa{s}"] = qa
            m[f"v{s}"] = np.ascontiguousarray(v.reshape(128, L // 128, 128).transpose(2, 1, 0))
        in_maps.append(m)
    res_d = _launch(("MD", LP, LS, l), lambda: KernMD(LP, LS, lambda_init).build(), in_maps, trace)
    mix = []
    for c in range(NCORE):
        yb = np.stack([np.concatenate([res_d[h]["o0"][:, c * TP:(c + 1) * TP], res_d[h]["o1"][:, c * TS:(c + 1) * TS]], 1)
                       for h in range(NCORE)], 0)
        ys = np.stack([np.concatenate([res_s[h]["y0"][:, c * TP:(c + 1) * TP], res_s[h]["y1"][:, c * TS:(c + 1) * TS]], 1)
                       for h in range(NCORE)], 0)
        mix.append({"yb_in": np.ascontiguousarray(yb), "ys5_in": np.ascontiguousarray(ys.astype(np.float32))})
    return mix


def make_consts():
    c = np.zeros((128, 384), np.float32)
    c[:, 0:128] = np.eye(128, dtype=np.float32)
    c[:, 128:256] = 1.0
    c[:, 256] = EPS
    c[:, 257] = np.pi / 2
    return c


def rope_consts():
    p = np.arange(128)
    j = p % 32
    invf = (10000.0 ** (-(2.0 * j) / 64.0)).astype(np.float32)
    sign = np.where((p % 64) < 32, -1.0, 1.0).astype(np.float32)
    perm = np.zeros((128, 128), np.float32)
    partner = np.where((p % 64) < 32, p + 32, p - 32)
    perm[partner, p] = 1.0
    return invf, sign, perm


def pos_table(t0, n):
    t = np.arange(t0, t0 + n)
    tab = np.empty((128, n), np.float32)
    tab[:64] = (t // 64)[None, :]
    tab[64:] = (t % 64)[None, :]
    return tab


_PROGS = {}


def _launch(key, builder, in_maps, trace=False):
    import time as _t
    _t0 = _t.time()
    if key not in _PROGS:
        _PROGS[key] = builder()
    nc = _PROGS[key]
    res = run_bass_kernel_spmd(nc, in_maps, core_ids=list(range(NCORE)), trace=trace)
    print("launch", key, "s", round(_t.time() - _t0, 1), "exec_ns", res.exec_time_ns, flush=True)
    return res.results


def run_model(inputs, LP, LS, layers, trace=False):
    import ml_dtypes
    bf = ml_dtypes.bfloat16
    TP, TS = LP // NCORE, LS // NCORE
    NTOK = TP + TS
    g = lambda n: np.asarray(inputs[n])
    xp, xs = g("x_prompt").reshape(LP, D), g("x_sample").reshape(LS, D)
    mem = np.ascontiguousarray(np.concatenate([g("mem_prompt").reshape(NMEM, D), g("mem_sample").reshape(NMEM, D)], 0))
    cst = make_consts()
    invf, sign, perm = rope_consts()
    nl = len(layers)
    xT = None
    mix = None
    for i in range(nl + 1):
        ops = []
        if i > 0:
            lp, kp = layers[i - 1]
            ops += ["mixout_even" if kp == "even" else "mixout_odd", "cross", "ffn2"]
        if i < nl:
            l, k = layers[i]
            ops += ["ffn1", "inproj_even" if k == "even" else "inproj_odd"]
        else:
            ops += ["final"]
        plan = {"first": i == 0, "ops": ops}
        small = np.zeros((128, 64), np.float32)
        small[:, 8], small[:, 9] = invf, sign
        shared = {"cst": cst, "perm": perm}
        if i > 0:
            shared["mem"] = mem
            for n in ("cross_w_q", "cross_w_kv", "cross_w_o", "cross_norm", "mem_norm", "ffn2_w_gu", "ffn2_w_down", "ffn2_norm"):
                shared[n] = np.ascontiguousarray(g(n)[lp])
            if kp == "even":
                e = lp // 2
                shared["mix_w_out"] = np.ascontiguousarray(g("even_w_out")[e])
                shared["s5_glu_w"] = np.ascontiguousarray(g("s5_glu_w")[e])
                small[:, 0:8] = g("s5_glu_b")[e].reshape(8, 128).T
            else:
                shared["mix_w_out"] = np.ascontiguousarray(g("odd_w_out")[lp // 2])
        if i < nl:
            for n in ("ffn1_w_gu", "ffn1_w_down", "ffn1_norm", "mix_norm"):
                shared[n] = np.ascontiguousarray(g(n)[l])
            if k == "even":
                shared["mix_w_in"] = np.ascontiguousarray(g("even_w_in")[l // 2])
            else:
                shared["mix_w_in"] = np.ascontiguousarray(g("odd_w_in")[l // 2])
                small[:, 11] = g("gqa_q_norm")[l // 2]
                small[:, 12] = g("gqa_k_norm")[l // 2]
        else:
            shared["final_norm"] = g("final_norm")
        shared["small"] = small
        in_maps = []
        for c in range(NCORE):
            m = dict(shared)
            if i == 0:
                m["x_tok"] = np.ascontiguousarray(np.concatenate([xp[c * TP:(c + 1) * TP], xs[c * TS:(c + 1) * TS]], 0))
            else:
                m["xT_in"] = xT[c]
                m.update(mix[c])
            if i < nl and k == "odd":
                m["pos_in"] = np.concatenate([pos_table(c * TP, TP), pos_table(c * TS, TS)], 1)
            in_maps.append(m)
        pl = plan

        def mk(pl=pl):
            kk = Kern(LP, LS)
            return kk.build(pl)
        res = _launch(("L", LP, LS, tuple(ops), i == 0), mk, in_maps, trace)
        if i == nl:
            yp = np.concatenate([res[c]["y_tok"][:TP] for c in range(NCORE)], 0).reshape(1, LP, D)
            ys = np.concatenate([res[c]["y_tok"][TP:] for c in range(NCORE)], 0).reshape(1, LS, D)
            return yp.astype(np.float32), ys.astype(np.float32)
        xT = [res[c]["xT_out"] for c in range(NCORE)]
        qkv = [np.concatenate([res[c]["qkv_out"][:, :, :TP] for c in range(NCORE)], 2),
               np.concatenate([res[c]["qkv_out"][:, :, TP:] for c in range(NCORE)], 2)]
        Ls = [LP, LS]
        if k == "odd":
            in_maps = []
            for c in range(NCORE):
                kh = c // 2
                h0 = 4 * kh + 2 * (c % 2)
                m = {}
                for s in range(2):
                    m[f"q{s}"] = np.ascontiguousarray(qkv[s][h0:h0 + 2])
                    m[f"k{s}"] = np.ascontiguousarray(qkv[s][16 + kh])
                    v = qkv[s][20 + kh]
                    m[f"v{s}"] = np.ascontiguousarray(v.reshape(128, Ls[s] // 128, 128).transpose(2, 1, 0))
                in_maps.append(m)
            res = _launch(("MO", LP, LS), lambda: KernMO(LP, LS).build(), in_maps, trace)
            mix = []
            o = [np.concatenate([res[c][f"o{s}"] for c in range(NCORE)], 0) for s in range(2)]
            for c in range(NCORE):
                yb = np.concatenate([o[0][:, :, c * TP:(c + 1) * TP], o[1][:, :, c * TS:(c + 1) * TS]], 2)
                mix.append({"yb_in": np.ascontiguousarray(yb)})
        else:
            u = [np.concatenate([res[c]["u_out"][:, :, :TP] for c in range(NCORE)], 2),
                 np.concatenate([res[c]["u_out"][:, :, TP:] for c in range(NCORE)], 2)]
            mix = run_even_mixer(inputs, l, LP, LS, qkv, u, trace)
    return None


def kernel(**inputs):
    layers = [(l, "even" if l % 2 == 0 else "odd") for l in range(4)]
    return run_model(inputs, 8192, 16384, layers)
```
